# Optimizing a Trainium2 kernel written in Bass

```python
import math
import jax, jax.numpy as jnp
from jax import lax
import numpy as np

D_MODEL = 1024
BATCH = 8
SEQ = 2048
DEPTH = 1

D_CONF = D_MODEL // 2
D_SC = D_MODEL // 2
GROUP = 128
CONF_KERNEL = 31
SC_KERNEL = 3
N_BRANCH = 2
D_FF = int(math.ceil(8 * D_MODEL / 3 / 256) * 256)
D_IN = 2 * D_CONF + 3 * D_SC + N_BRANCH * D_MODEL
RMS_EPS = 1e-6
LN_EPS = 1e-5

kernel_name = "hybrid_conformer_shortconv_gated_adaln_block"


def _rmsnorm(x, g):
    xf = x.astype(jnp.float32)
    y = xf * lax.rsqrt(jnp.mean(xf * xf, axis=-1, keepdims=True) + RMS_EPS)
    return (y * g.astype(jnp.float32)).astype(x.dtype)


def _layernorm(x, g, b):
    xf = x.astype(jnp.float32)
    mu = jnp.mean(xf, axis=-1, keepdims=True)
    var = jnp.mean(jnp.square(xf - mu), axis=-1, keepdims=True)
    y = (xf - mu) * lax.rsqrt(var + LN_EPS)
    return (y * g.astype(jnp.float32) + b.astype(jnp.float32)).astype(x.dtype)


def _causal_dwconv(x, w):
    k, ch = w.shape
    rhs = w.astype(x.dtype)[:, None, :]
    return lax.conv_general_dilated(
        x, rhs, window_strides=(1,), padding=[(k - 1, 0)],
        dimension_numbers=("NWC", "WIO", "NWC"), feature_group_count=ch)


def _modulate(h, shift, scale):
    return h * (1.0 + scale[:, None, :]) + shift[:, None, :]


def setup_inputs(seed: int = 0) -> dict:
    key = jax.random.key(seed)
    ks = jax.random.split(key, 24)
    L, D = DEPTH, D_MODEL
    n = lambda k, shape, s: (jax.random.normal(k, shape, jnp.float32) * s)
    return {
        "x": n(ks[0], (BATCH, SEQ, D), 1.0),
        "c": n(ks[1], (BATCH, D), 1.0),
        "w_ada": n(ks[2], (L, D, 6 * D), 0.3 * D ** -0.5),
        "b_ada": n(ks[3], (L, 6 * D), 0.02),
        "norm1_g": 1.0 + n(ks[4], (L, D), 0.02),
        "w_in": n(ks[5], (L, D, D_IN), D ** -0.5),
        "b_glu": n(ks[6], (L, 2 * D_CONF), 0.02),
        "conf_dw_w": n(ks[7], (L, CONF_KERNEL, D_CONF), CONF_KERNEL ** -0.5),
        "conf_dw_b": n(ks[8], (L, D_CONF), 0.02),
        "conf_ln_g": 1.0 + n(ks[9], (L, D_CONF), 0.02),
        "conf_ln_b": n(ks[10], (L, D_CONF), 0.02),
        "conf_w_out": n(ks[11], (L, D_CONF, D), D_CONF ** -0.5),
        "conf_b_out": n(ks[12], (L, D), 0.02),
        "sc_dw_w": n(ks[13], (L, SC_KERNEL, D_SC), SC_KERNEL ** -0.5),
        "sc_w_out": n(ks[14], (L, D_SC, D), D_SC ** -0.5),
        "b_gate": n(ks[15], (L, N_BRANCH * D), 0.02),
        "w_o": n(ks[16], (L, D, D), D ** -0.5),
        "norm2_g": 1.0 + n(ks[17], (L, D), 0.02),
        "w_gu": n(ks[18], (L, D, 2 * D_FF), D ** -0.5),
        "w_down": n(ks[19], (L, D_FF, D), D_FF ** -0.5),
        "final_g": 1.0 + n(ks[20], (D,), 0.02),
    }


def reference(x, c, w_ada, b_ada, norm1_g, w_in, b_glu, conf_dw_w, conf_dw_b,
              conf_ln_g, conf_ln_b, conf_w_out, conf_b_out, sc_dw_w, sc_w_out,
              b_gate, w_o, norm2_g, w_gu, w_down, final_g):
    c_act = jax.nn.silu(c)
    for l in range(DEPTH):
        mod = c_act @ w_ada[l] + b_ada[l]
        sh1, sc1, gt1, sh2, sc2, gt2 = jnp.split(mod, 6, axis=-1)

        h = _modulate(_rmsnorm(x, norm1_g[l]), sh1, sc1)
        p = h @ w_in[l]
        conf_in, sc_in, gate_in = jnp.split(
            p, [2 * D_CONF, 2 * D_CONF + 3 * D_SC], axis=-1)

        u = conf_in + b_glu[l]
        u_a, u_b = jnp.split(u, 2, axis=-1)
        u = u_a * jax.nn.sigmoid(u_b)
        u = _causal_dwconv(u, conf_dw_w[l]) + conf_dw_b[l]
        u = jax.nn.silu(_layernorm(u, conf_ln_g[l], conf_ln_b[l]))
        y_a = u @ conf_w_out[l] + conf_b_out[l]

        g_b, g_c, v = jnp.split(sc_in, 3, axis=-1)
        v = g_b * _causal_dwconv(g_c * v, sc_dw_w[l])
        y_b = v @ sc_w_out[l]

        gates = jax.nn.sigmoid(gate_in + b_gate[l])
        ga, gb = jnp.split(gates, 2, axis=-1)
        mixed = (ga * y_a + gb * y_b) @ w_o[l]
        x = x + gt1[:, None, :] * mixed

        h2 = _modulate(_rmsnorm(x, norm2_g[l]), sh2, sc2)
        gu = h2 @ w_gu[l]
        f_g, f_u = jnp.split(gu, 2, axis=-1)
        ffn = (jax.nn.silu(f_g) * f_u) @ w_down[l]
        x = x + gt2[:, None, :] * ffn

    return _rmsnorm(x, final_g)
```

```python
from contextlib import ExitStack
import numpy as np
import concourse.bass as bass
import concourse.mybir as mybir
from concourse.bass_utils import run_bass_kernel_spmd

F32 = mybir.dt.float32
BF16 = mybir.dt.bfloat16
ALU = mybir.AluOpType
AF = mybir.ActivationFunctionType

ENGS = ("pe", "act", "dve", "pool", "sp")
NCORES = 8
S = 2048
TU = 512
NR = 4
KD = 8
KF = 22
NV = 256
FGROUPS = [(0, 4), (4, 10), (10, 16), (16, 22)]

C_BADA, C_G1, C_BGLU, C_DWB, C_LNG, C_LNB, C_CBO, C_BGATE, C_G2, C_GF, C_DWW, C_SCW, C_EPSR, C_EPSL = (
    0, 48, 56, 64, 68, 72, 76, 84, 100, 108, 116, 240, 252, 253)


class Buf:
    __slots__ = ("name", "w", "r", "lo", "hi", "ov", "excl")

    def __init__(self, name, lo=None, hi=None):
        self.name = name
        self.excl = False
        self.w = None
        self.r = {}
        self.lo = lo
        self.hi = hi
        self.ov = []


class Sched:
    def __init__(self, nc):
        self.nc = nc
        self.ops = {e: [] for e in ENGS}
        self.cnt = {e: 0 for e in ENGS}
        self.known = {e: {} for e in ENGS}
        self.sems = {}
        self.dma_cnt = {}
        self.ibufs = []

    def add_sem(self, key, handle):
        self.sems[key] = handle

    def buf(self, name, lo=None, hi=None):
        b = Buf(name, lo, hi)
        if lo is not None:
            for o in self.ibufs:
                if o.lo < hi and lo < o.hi:
                    o.ov.append(b)
                    b.ov.append(o)
            self.ibufs.append(b)
        return b

    def _need(self, eng, tok):
        if tok is None:
            return
        key, val = tok
        if key == "pe" and eng == "pe":
            return
        if self.known[eng].get(key, 0) >= val:
            return
        self.known[eng][key] = val
        h = self.sems[key]
        self.ops[eng].append(lambda e, h=h, val=val: e.wait_ge(h, val))

    def deps(self, eng, reads, writes):
        for b in reads:
            self._need(eng, b.w)
            for o in b.ov:
                self._need(eng, o.w)
            if b.excl:
                for k, v in b.r.items():
                    if k != eng:
                        self._need(eng, (k, v))
        for b in writes:
            self._need(eng, b.w)
            for k, v in b.r.items():
                self._need(eng, (k, v))
            for o in b.ov:
                self._need(eng, o.w)
                for k, v in o.r.items():
                    self._need(eng, (k, v))

    def _mark(self, tok, reads, writes):
        for b in reads:
            if b.r.get(tok[0], 0) < tok[1]:
                b.r[tok[0]] = tok[1]
        for b in writes:
            b.w = tok
            b.r = {}

    def op(self, eng, fn, reads=(), writes=()):
        self.deps(eng, reads, writes)
        self.cnt[eng] += 1
        tok = (eng, self.cnt[eng])
        h = self.sems[eng]
        self.ops[eng].append(lambda e, fn=fn, h=h: fn(e).then_inc(h, 1))
        self._mark(tok, reads, writes)

    def pe_group(self, fns, reads=(), writes=()):
        self.deps("pe", reads, writes)
        self.cnt["pe"] += 1
        tok = ("pe", self.cnt["pe"])
        h = self.sems["pe"]
        for f in fns[:-1]:
            self.ops["pe"].append(lambda e, f=f: f(e))
        self.ops["pe"].append(lambda e, f=fns[-1], h=h: f(e).then_inc(h, 1))
        self._mark(tok, reads, writes)

    def dma(self, eng, semkey, fn, reads=(), writes=()):
        self.deps(eng, reads, writes)
        self.dma_cnt[semkey] = self.dma_cnt.get(semkey, 0) + 16
        tok = (semkey, self.dma_cnt[semkey])
        h = self.sems[semkey]
        self.ops[eng].append(lambda e, fn=fn, h=h: fn(e).then_inc(h, 16))
        self._mark(tok, reads, writes)
        return tok

    def wait_tok(self, eng, tok):
        self._need(eng, tok)

    def emit(self):
        with self.nc.Block() as block:
            @block.tensor
            def _(e):
                for f in self.ops["pe"]:
                    f(e)

            @block.scalar
            def _(e):
                for f in self.ops["act"]:
                    f(e)

            @block.vector
            def _(e):
                for f in self.ops["dve"]:
                    f(e)

            @block.gpsimd
            def _(e):
                for f in self.ops["pool"]:
                    f(e)

            @block.sync
            def _(e):
                for f in self.ops["sp"]:
                    f(e)


_DBG = {}


class _Stop(Exception):
    pass


def build_nc(stop=None):
    nc = bass.Bass("TRN2", target_bir_lowering=False)
    din = lambda n, s: nc.dram_tensor(n, s, F32, kind="ExternalInput").ap()
    xT = din("xT", [1024, S])
    cb_d = din("cb", [128, 1024])
    wada_d = din("wada", [128, 48 * 1024])
    vecs_d = din("vecs", [128, NV])
    cst_d = din("cst", [128, 384])
    w8_d = din("w8", [22, 128, 4096])
    w4_d = din("w4", [2, 128, 4096])
    wd_d = din("wd", [8, 128, 3072])
    out_d = nc.dram_tensor("outT", [1024, S], F32, kind="ExternalOutput").ap()

    with ExitStack() as st:
        NARENA = 53200
        arena = st.enter_context(nc.sbuf_tensor("arena", [128, NARENA], F32))
        ps = st.enter_context(nc.psum_tensor("ps", [128, 8 * TU], F32))
        sch = Sched(nc)
        for e in ENGS:
            sch.add_sem(e, st.enter_context(nc.semaphore("s_" + e)))

        def dsem(name):
            sch.add_sem(name, st.enter_context(nc.semaphore(name)))
            return name

        def f32ap(off, n):
            assert off % 4 == 0 and off + 4 * n <= NARENA * 4, (off, n)
            return arena[:, off // 4: off // 4 + n]

        def bfap(off, n):
            assert off % 4 == 0 and n % 2 == 0 and off + 2 * n <= NARENA * 4, (off, n)
            return arena[:, off // 4: off // 4 + n // 2].bitcast(BF16)

        O_XS, O_H, O_VEC, O_CST, O_MOD, O_GM = 0, 65536, 98304, 99328, 100096, 100288
        M = 100352
        xs = f32ap(O_XS, KD * S)
        hbuf = bfap(O_H, KD * S)
        vecs = f32ap(O_VEC, NV)
        cst = bfap(O_CST, 384)
        mod = f32ap(O_MOD, 48)
        gm = f32ap(O_GM, 16)
        ident, ones_d, ones_c = cst[:, 0:128], cst[:, 128:256], cst[:, 256:384]

        B_xs = [[sch.buf("xs%d_%d" % (k, r), O_XS + (k * S + r * TU) * 4, O_XS + (k * S + (r + 1) * TU) * 4)
                 for r in range(NR)] for k in range(KD)]
        B_h = [[sch.buf("h%d_%d" % (k, r), O_H + (k * S + r * TU) * 2, O_H + (k * S + (r + 1) * TU) * 2)
                for r in range(NR)] for k in range(KD)]
        B_vec = sch.buf("vecs", O_VEC, O_VEC + NV * 4)
        B_cst = sch.buf("cst", O_CST, O_CST + 768)
        B_mod = sch.buf("mod", O_MOD, O_MOD + 192)
        B_gm = sch.buf("gm", O_GM, O_GM + 64)

        def mbuf(name, off, nbytes):
            return sch.buf(name, M + off, M + off + nbytes)

        SLOT_OFF = {"A": 0, "B": 8192, "C": 65536, "D": 73728, "CWO": 97536, "SWO": 49280, "E": 41088}
        slot_ap = {k: bfap(M + v, 4096) for k, v in SLOT_OFF.items()}
        slot_buf = {k: mbuf("slot" + k, v, 8192) for k, v in SLOT_OFF.items()}
        slot_sem = {k: dsem("d_slot" + k) for k in SLOT_OFF}

        O_DIAG, O_U, O_S = 16384, 48128, 64768
        UW = 32 + S
        diag = bfap(M + O_DIAG, 4 * 31 * 128)
        B_diag = [mbuf("diag%d" % j, O_DIAG + j * 31 * 256, 31 * 256) for j in range(4)]
        upad = bfap(M + O_U, 4 * UW)
        B_upad0 = [mbuf("upad%d" % j, O_U + j * UW * 2, 64) for j in range(4)]
        B_u = [[mbuf("u%d_%d" % (j, r), O_U + (j * UW + 32 + r * TU) * 2, TU * 2) for r in range(NR)] for j in range(4)]
        sbuf_s = bfap(M + O_S, 4 * S)
        B_s = [[mbuf("s%d_%d" % (j, r), O_S + (j * S + r * TU) * 2, TU * 2) for r in range(NR)] for j in range(4)]
        O_C32, O_CBF, O_CSQ, O_MEAN, O_SQV2, O_RSTD2, O_T1, O_SIG = 81152, 89344, 93440, 97536, 99584, 101632, 103680, 107776
        c32 = f32ap(M + O_C32, 4 * TU); B_c32 = [mbuf("c32_%d" % j, O_C32 + j * 2048, 2048) for j in range(4)]
        cbf = bfap(M + O_CBF, 4 * TU); B_cbf = [mbuf("cbf%d" % j, O_CBF + j * 1024, 1024) for j in range(4)]
        csq = bfap(M + O_CSQ, 4 * TU); B_csq = [mbuf("csq%d" % j, O_CSQ + j * 1024, 1024) for j in range(4)]
        mean_sb = f32ap(M + O_MEAN, TU); B_mean = mbuf("mean", O_MEAN, 2048)
        sqv2 = f32ap(M + O_SQV2, TU); B_sqv2 = mbuf("sqv2", O_SQV2, 2048)
        rstd2 = f32ap(M + O_RSTD2, TU); B_rstd2 = mbuf("rstd2", O_RSTD2, 2048)
        t1 = [f32ap(M + O_T1 + i * 2048, TU) for i in range(2)]; B_t1 = [mbuf("t1_%d" % i, O_T1 + i * 2048, 2048) for i in range(2)]
        sig = [f32ap(M + O_SIG + i * 2048, TU) for i in range(2)]; B_sig = [mbuf("sig%d" % i, O_SIG + i * 2048, 2048) for i in range(2)]
        O_WADA, O_CACT, O_JUNK, O_CB = 64768, 97536, 101632, 105728
        NWR = 4
        WADA_OFF = [O_WADA, O_WADA + 8192, O_DIAG, O_DIAG + 8192]
        wada = [f32ap(M + WADA_OFF[i], 2048) for i in range(NWR)]
        B_wada = [mbuf("wada%d" % i, WADA_OFF[i], 8192) for i in range(NWR)]
        wada_sem = [dsem("d_wada%d" % i) for i in range(NWR)]
        cact = f32ap(M + O_CACT, 1024); B_cact = mbuf("cact", O_CACT, 4096)
        junk = f32ap(M + O_JUNK, 1024); B_junk = mbuf("junk", O_JUNK, 4096)
        cbt = f32ap(M + O_CB, 1024); B_cb = mbuf("cb", O_CB, 4096)
        class NT:
            pass

        def norm_temps(base, sqbase, tag):
            t = NT()
            t.sq = [bfap(M + sqbase + i * 1024, TU) for i in range(8)]
            t.B_sq = [mbuf(tag + "sq%d" % i, sqbase + i * 1024, 1024) for i in range(8)]
            t.rstd = [f32ap(M + base + i * 2048, TU) for i in range(2)]
            t.B_rstd = [mbuf(tag + "rstd%d" % i, base + i * 2048, 2048) for i in range(2)]
            t.tmp = [f32ap(M + base + 4096 + i * 2048, TU) for i in range(2)]
            t.B_tmp = [mbuf(tag + "tmp%d" % i, base + 4096 + i * 2048, 2048) for i in range(2)]
            return t

        NT_U = norm_temps(81152, 89344, "u")
        NT_N = norm_temps(94208, 86016, "n")
        NOT = 8
        ot_i = {"i": 0}
        otile_sem = [dsem("d_ot%d" % i) for i in range(NOT)]
        O_OT = 16384
        otile = [f32ap(M + O_OT + i * 2048, TU) for i in range(NOT)]; B_ot = [mbuf("ot%d" % i, O_OT + i * 2048, 2048) for i in range(NOT)]
        O_CVP, O_GCS, O_ACC, O_V = 16384, 32896, 36992, 81152
        CW = 2 + S
        cvp = [f32ap(M + O_CVP + i * 8256, CW) for i in range(2)]
        B_cvp0 = [mbuf("cvp0_%d" % i, O_CVP + i * 8256, 8) for i in range(2)]
        B_cvp = [[mbuf("cvp%d_%d" % (i, r), O_CVP + i * 8256 + (2 + r * TU) * 4, TU * 4) for r in range(NR)] for i in range(2)]
        gcs = [f32ap(M + O_GCS + i * 2048, TU) for i in range(2)]; B_gcs = [mbuf("gcs%d" % i, O_GCS + i * 2048, 2048) for i in range(2)]
        acc = [f32ap(M + O_ACC + i * 2048, TU) for i in range(2)]; B_acc = [mbuf("acc%d" % i, O_ACC + i * 2048, 2048) for i in range(2)]
        vbuf = bfap(M + O_V, 4 * S)
        B_v = [[mbuf("v%d_%d" % (j, r), O_V + (j * S + r * TU) * 2, TU * 2) for r in range(NR)] for j in range(4)]
        O_MIX, O_GSB, O_GY = 16384, 57472, 105728
        mixed = bfap(M + O_MIX, KD * S)
        B_mix = [[mbuf("mix%d_%d" % (m, r), O_MIX + (m * S + r * TU) * 2, TU * 2) for r in range(NR)] for m in range(KD)]
        gsb = [f32ap(M + O_GSB + i * 2048, TU) for i in range(2)]; B_gsb = [mbuf("gsb%d" % i, O_GSB + i * 2048, 2048) for i in range(2)]
        gy = [f32ap(M + O_GY + i * 2048, TU) for i in range(2)]; B_gy = [mbuf("gy%d" % i, O_GY + i * 2048, 2048) for i in range(2)]
        O_AB, O_SG = 16384, 81920
        abuf = [bfap(M + O_AB + g * 6 * 4096, 6 * S) for g in range(2)]
        B_ab = [[[mbuf("ab%d_%d_%d" % (g, jj, r), O_AB + g * 24576 + (jj * S + r * TU) * 2, TU * 2) for r in range(NR)]
                 for jj in range(6)] for g in range(2)]
        sg = [f32ap(M + O_SG + i * 2048, TU) for i in range(2)]; B_sg = [mbuf("sg%d" % i, O_SG + i * 2048, 2048) for i in range(2)]

        bank = [ps[:, b * TU:(b + 1) * TU] for b in range(8)]
        B_bank = [sch.buf("bank%d" % b) for b in range(8)]
        for bb in B_bank:
            bb.excl = True
        ring = {"i": 0}

        def next_bank():
            b = ring["i"] % 6
            ring["i"] += 1
            return b

        def R(r):
            return slice(r * TU, (r + 1) * TU)

        def vcol(c):
            return vecs[:, c:c + 1]

        def ACT(out, in_, func, reads, writes, bias=None, scale=None):
            kw = {}
            if bias is not None:
                kw["bias"] = bias
            if scale is not None:
                kw["scale"] = scale
            sch.op("act", lambda e: e.activation(out=out, in_=in_, func=func, **kw), reads=reads, writes=writes)

        def TT(out, in0, in1, op, reads, writes, eng="dve"):
            sch.op(eng, lambda e: e.tensor_tensor(out=out, in0=in0, in1=in1, op=op), reads=reads, writes=writes)

        def STT(out, in0, scalar, in1, op0, op1, reads, writes, accum_out=None):
            if accum_out is None:
                sch.op("dve", lambda e: e.scalar_tensor_tensor(out=out, in0=in0, scalar=scalar, in1=in1, op0=op0, op1=op1),
                       reads=reads, writes=writes)
            else:
                sch.op("dve", lambda e: e.scalar_tensor_tensor(out=out, in0=in0, scalar=scalar, in1=in1, op0=op0, op1=op1,
                                                               accum_out=accum_out), reads=reads, writes=writes)

        def TS(out, in0, scalar1, op0, reads, writes):
            sch.op("dve", lambda e: e.tensor_scalar(out=out, in0=in0, scalar1=scalar1, scalar2=None, op0=op0),
                   reads=reads, writes=writes)

        def MM(b, pairs, reads):
            fns = []
            n = len(pairs)
            for i, (lt, rh) in enumerate(pairs):
                fns.append(lambda e, lt=lt, rh=rh, i=i: e.matmul(bank[b], lhsT=lt, rhs=rh, start=(i == 0), stop=(i == n - 1)))
            sch.pe_group(fns, reads=reads, writes=[B_bank[b]])

        def load_slot(slot, src, ncols=4096):
            dst = slot_ap[slot][:, 0:ncols]
            return sch.dma("pool", slot_sem[slot], lambda e: e.dma_start(out=dst, in_=src), writes=[slot_buf[slot]])

        s_vec, s_cst, s_cb = dsem("d_vec"), dsem("d_cst"), dsem("d_cb")
        s_x = [dsem("d_x%d" % r) for r in range(NR)]
        sch.dma("sp", s_vec, lambda e: e.dma_start(out=vecs, in_=vecs_d), writes=[B_vec])
        sch.dma("sp", s_cb, lambda e: e.dma_start(out=cbt, in_=cb_d), writes=[B_cb])
        sch.dma("pool", s_cst, lambda e: e.dma_start(out=cst, in_=cst_d), writes=[B_cst])
        stage = [f32ap(O_H + i * 16384, 4096) for i in range(2)]
        B_stage = [sch.buf("stage%d" % i, O_H + i * 16384, O_H + (i + 1) * 16384) for i in range(2)]
        s_stage = [dsem("d_stage%d" % i) for i in range(2)]
        for i in range(2):
            sch.dma("sp", s_stage[i], lambda e, i=i: e.dma_start(out=stage[i], in_=w8_d[i]), writes=[B_stage[i]])
        xs3 = xs.rearrange("p (k t) -> p k t", k=KD)
        xT3 = xT.rearrange("(k p) t -> p k t", p=128)
        outT3 = out_d.rearrange("(k p) t -> p k t", p=128)

        def load_x(r, eng="sp"):
            sch.dma(eng, s_x[r], lambda e: e.dma_start(out=xs3[:, :, R(r)], in_=xT3[:, :, R(r)]),
                    writes=[B_xs[k][r] for k in range(KD)])

        load_x(0)

        def build_diag(j):
            dj = diag[:, j * 31 * 128:(j + 1) * 31 * 128]
            wj = vecs[:, C_DWW + j * 31: C_DWW + (j + 1) * 31]
            o3 = bass.AP(dj.tensor, dj.offset, [list(dj.ap[0]), [128, 31], [1, 128]])
            i3 = bass.AP(ident.tensor, ident.offset, [list(ident.ap[0]), [0, 31], [1, 128]])
            w3 = bass.AP(wj.tensor, wj.offset, [list(wj.ap[0]), [1, 31], [0, 128]])
            sch.op("dve", lambda e, o3=o3, i3=i3, w3=w3: e.tensor_tensor(out=o3, in0=i3, in1=w3, op=ALU.mult),
                   reads=[B_cst, B_vec], writes=[B_diag[j]])

        ACT(cact, cbt, AF.Silu, [B_cb], [B_cact])
        for i, slot in enumerate(("A", "B")):
            for hh in range(4):
                ACT(slot_ap[slot][:, hh * 1024:(hh + 1) * 1024], stage[i][:, hh * 1024:(hh + 1) * 1024], AF.Identity,
                    [B_stage[i]], [slot_buf[slot]])
        sch.op("dve", lambda e: e.memset(mod, 0.0), writes=[B_mod])
        wada_i = {"i": 0}
        ada_tok = {}

        def ada_group(g):
            i = wada_i["i"] % NWR
            wada_i["i"] += 1
            ada_tok[g] = sch.dma("sp", wada_sem[i], lambda e: e.dma_start(out=wada[i], in_=wada_d[:, g * 2048:(g + 1) * 2048]),
                    writes=[B_wada[i]])
            for q in range(2):
                n = g * 2 + q
                STT(junk, wada[i][:, q * 1024:(q + 1) * 1024], 1.0, cact, ALU.mult, ALU.mult,
                    [B_wada[i], B_cact], [B_junk, B_mod], accum_out=mod[:, n:n + 1])

        def ada_finish(lo, hi):
            TT(mod[:, lo:hi], mod[:, lo:hi], vecs[:, C_BADA + lo:C_BADA + hi], ALU.add, [B_mod, B_vec], [B_mod])

        B_sh = [sch.buf("sh1_%d" % k, O_MOD + k * 4, O_MOD + k * 4 + 4) for k in range(KD)]
        B_sc = [sch.buf("sc1_%d" % k, O_MOD + (8 + k) * 4, O_MOD + (8 + k) * 4 + 4) for k in range(KD)]
        B_gmc = [sch.buf("gm1_%d" % k, O_GM + k * 4, O_GM + k * 4 + 4) for k in range(KD)]
        wada3_d = wada_d.rearrange("p (n k) -> p n k", k=1024)

        def crit_group(k):
            i = wada_i["i"] % NWR
            wada_i["i"] += 1
            dst = wada[i].rearrange("p (n k) -> p n k", n=2)
            sch.dma("sp", wada_sem[i], lambda e, i=i, k=k, dst=dst: e.dma_start(out=dst, in_=wada3_d[:, k:k + 9:8, :]),
                    writes=[B_wada[i]])
            STT(junk, wada[i][:, 0:1024], 1.0, cact, ALU.mult, ALU.mult, [B_wada[i], B_cact], [B_junk, B_sh[k]],
                accum_out=mod[:, k:k + 1])
            STT(junk, wada[i][:, 1024:2048], 1.0, cact, ALU.mult, ALU.mult, [B_wada[i], B_cact], [B_junk, B_sc[k]],
                accum_out=mod[:, 8 + k:9 + k])
            TT(mod[:, k:k + 1], mod[:, k:k + 1], vecs[:, C_BADA + k:C_BADA + k + 1], ALU.add, [B_sh[k], B_vec], [B_sh[k]])
            TT(mod[:, 8 + k:9 + k], mod[:, 8 + k:9 + k], vecs[:, C_BADA + 8 + k:C_BADA + 9 + k], ALU.add,
               [B_sc[k], B_vec], [B_sc[k]])
            sch.op("dve", lambda e, k=k: e.scalar_tensor_tensor(out=gm[:, k:k + 1], in0=mod[:, 8 + k:9 + k], scalar=1.0,
                                                                in1=vecs[:, C_G1 + k:C_G1 + k + 1], op0=ALU.add, op1=ALU.mult),
                   reads=[B_sc[k], B_vec], writes=[B_gmc[k]])

        def norm_sq(r, mode):
            t = NT_U if mode == 1 else NT_N
            for k in range(KD):
                ACT(t.sq[k], xs[:, k * S + r * TU: k * S + (r + 1) * TU], AF.Square, [B_xs[k][r]], [t.B_sq[k]])

        def norm_red(r, mode):
            t = NT_U if mode == 1 else NT_N
            fns = [lambda e, k=k, t=t: e.matmul(bank[6], lhsT=ones_d, rhs=t.sq[k], start=(k == 0), stop=(k == KD - 1))
                   for k in range(KD)]
            sch.pe_group(fns, reads=list(t.B_sq) + [B_cst], writes=[B_bank[6]])
            ACT(t.rstd[r % 2], bank[6], AF.Ln, [B_bank[6], B_vec], [t.B_rstd[r % 2]], bias=vcol(C_EPSR))
            ACT(t.rstd[r % 2], t.rstd[r % 2], AF.Exp, [t.B_rstd[r % 2]], [t.B_rstd[r % 2]], scale=-0.5)

        def norm_stats(r, mode):
            norm_sq(r, mode)
            norm_red(r, mode)

        def norm_apply(r, mode, ks=None):
            t = NT_U if mode == 1 else NT_N
            rstd, B_rstd = t.rstd[r % 2], t.B_rstd[r % 2]
            for k in (range(KD) if ks is None else ks):
                i = k % 2
                xk = xs[:, k * S + r * TU: k * S + (r + 1) * TU]
                if mode == "final":
                    o = ot_i["i"] % NOT
                    ot_i["i"] += 1
                    STT(otile[o], xk, vcol(C_GF + k), rstd, ALU.mult, ALU.mult, [B_xs[k][r], B_rstd, B_vec], [B_ot[o]])
                    sch.dma("sp", otile_sem[o], lambda e, o=o, k=k: e.dma_start(out=outT3[:, k, R(r)], in_=otile[o]),
                            reads=[B_ot[o]])
                else:
                    gcol, shcol = (0, 0) if mode == 1 else (8, 24)
                    hk_ = hbuf[:, k * S + r * TU: k * S + (r + 1) * TU]
                    if mode == 1 or i == 1:
                        STT(t.tmp[i], xk, gm[:, gcol + k: gcol + k + 1], rstd, ALU.mult, ALU.mult,
                            [B_xs[k][r], B_rstd, B_gm], [t.B_tmp[i]])
                        TS(hk_, t.tmp[i], mod[:, shcol + k: shcol + k + 1], ALU.add, [t.B_tmp[i], B_mod], [B_h[k][r]])
                    else:
                        TT(t.tmp[i], xk, rstd, ALU.mult, [B_xs[k][r], B_rstd], [t.B_tmp[i]])
                        ACT(hk_, t.tmp[i], AF.Identity, [t.B_tmp[i], B_gm, B_mod], [B_h[k][r]],
                            bias=mod[:, shcol + k: shcol + k + 1], scale=gm[:, gcol + k: gcol + k + 1])

        def rms_norm(r, mode):
            norm_stats(r, mode)
            norm_apply(r, mode)

        def norm_pipeline(units_fn, mode, fillers=None):
            units_fn(0)
            norm_sq(0, mode)
            units_fn(1)
            norm_red(0, mode)
            norm_sq(1, mode)
            for r in range(2, NR):
                units_fn(r)
                norm_apply(r - 2, mode)
                norm_red(r - 1, mode)
                norm_sq(r, mode)
            if fillers:
                fillers[0]()
            norm_apply(NR - 2, mode)
            if fillers:
                st_k = {"k": 0, "u": 0, "red": False}

                def one_k():
                    if st_k["k"] < KD:
                        norm_apply(NR - 1, mode, ks=[st_k["k"]])
                        st_k["k"] += 1

                def after_unit():
                    st_k["u"] += 1
                    if st_k["u"] == 2:
                        norm_red(NR - 1, mode)
                        st_k["red"] = True
                    elif st_k["red"]:
                        one_k()
                        if st_k["u"] >= 7:
                            one_k()
                fillers[1](after_unit)
                if not st_k["red"]:
                    norm_red(NR - 1, mode)
                while st_k["k"] < KD:
                    one_k()
            else:
                norm_red(NR - 1, mode)
                norm_apply(NR - 1, mode)

        def checkpoint(n):
            if stop is not None and n == stop:
                raise _Stop()

        def stages():
            checkpoint(0)
            norm_sq(0, 1)
            norm_red(0, 1)
            checkpoint(1)

            def hk(k, r):
                return hbuf[:, k * S + r * TU: k * S + (r + 1) * TU]

            def w8pairs(slot, blk, r):
                return [(slot_ap[slot][:, kt * 512 + blk * 128: kt * 512 + (blk + 1) * 128], hk(kt, r)) for kt in range(KD)]

            def h_reads(r):
                return [B_h[k][r] for k in range(KD)]

            ada_pending = []

            def ada_some(n):
                for _ in range(n):
                    if ada_pending:
                        ada_group(ada_pending.pop(0))

            for jj in range(4):
                sch.op("pool", lambda e, jj=jj: e.memset(upad[:, jj * UW: jj * UW + 32], 0.0), writes=[B_upad0[jj]])
            tU = NT_U
            for k in range(KD):
                crit_group(k)
                i = k % 2
                xk = xs[:, k * S: k * S + TU]
                TT(tU.tmp[i], xk, tU.rstd[0], ALU.mult, [B_xs[k][0], tU.B_rstd[0]], [tU.B_tmp[i]])
                ACT(hk(k, 0), tU.tmp[i], AF.Identity, [tU.B_tmp[i], B_gmc[k], B_sh[k]], [B_h[k][0]],
                    bias=mod[:, k:k + 1], scale=gm[:, k:k + 1])
                for blk8 in range(6):
                    slot = "A" if blk8 < 4 else "B"
                    lt = slot_ap[slot][:, k * 512 + (blk8 % 4) * 128: k * 512 + (blk8 % 4 + 1) * 128]
                    sch.pe_group([lambda e, blk8=blk8, lt=lt, k=k: e.matmul(bank[blk8], lhsT=lt, rhs=hk(k, 0),
                                                                          start=(k == 0), stop=(k == KD - 1))],
                                 reads=[B_h[k][0], slot_buf[slot]], writes=[B_bank[blk8]])
                if k == 2:
                    load_x(1)
                if k == 5:
                    norm_sq(1, 1)
                    norm_red(1, 1)
            load_x(2)
            load_x(3)
            norm_apply(1, 1)
            cnt = 0
            for j in range(3):
                i = cnt % 2
                cnt += 1
                ACT(sig[i], bank[2 * j], AF.Sigmoid, [B_bank[2 * j], B_vec], [B_sig[i]], bias=vcol(C_BGLU + 4 + j))
                STT(upad[:, j * UW + 32: j * UW + 32 + TU], bank[2 * j + 1], vcol(C_BGLU + j), sig[i], ALU.add, ALU.mult,
                    [B_bank[2 * j + 1], B_sig[i], B_vec], [B_u[j][0]])
            b1 = next_bank()
            MM(b1, w8pairs("B", 2, 0), h_reads(0) + [slot_buf["B"]])
            b2 = next_bank()
            MM(b2, w8pairs("B", 3, 0), h_reads(0) + [slot_buf["B"]])
            i = cnt % 2
            cnt += 1
            ACT(sig[i], bank[b1], AF.Sigmoid, [B_bank[b1], B_vec], [B_sig[i]], bias=vcol(C_BGLU + 4 + 3))
            STT(upad[:, 3 * UW + 32: 3 * UW + 32 + TU], bank[b2], vcol(C_BGLU + 3), sig[i], ALU.add, ALU.mult,
                [B_bank[b2], B_sig[i], B_vec], [B_u[3][0]])
            norm_sq(2, 1)
            for _ in range(3):
                ada_some(1)
            for r in range(1, NR):
                for j in range(4):
                    slot = "A" if j < 2 else "B"
                    blk = (j % 2) * 2
                    b1 = next_bank()
                    MM(b1, w8pairs(slot, blk, r), h_reads(r) + [slot_buf[slot]])
                    b2 = next_bank()
                    MM(b2, w8pairs(slot, blk + 1, r), h_reads(r) + [slot_buf[slot]])
                    i = cnt % 2
                    cnt += 1
                    ACT(sig[i], bank[b1], AF.Sigmoid, [B_bank[b1], B_vec], [B_sig[i]], bias=vcol(C_BGLU + 4 + j))
                    STT(upad[:, j * UW + 32 + r * TU: j * UW + 32 + (r + 1) * TU], bank[b2], vcol(C_BGLU + j), sig[i], ALU.add, ALU.mult,
                        [B_bank[b2], B_sig[i], B_vec], [B_u[j][r]])
                    if j < 3:
                        ada_some(1)
                    if j == 0 and r + 1 < NR:
                        norm_red(r + 1, 1)
                        norm_apply(r + 1, 1)
                    if j == 3 and r + 2 < NR:
                        norm_sq(r + 2, 1)
                    if j == 3:
                        if r == 1:
                            build_diag(0)
                            build_diag(1)
                        else:
                            build_diag(r)

            checkpoint(2)
            load_slot("A", w8_d[2])
            load_slot("B", w8_d[3])

            def conv_stats(j):
                sch.pe_group([lambda e, j=j: e.matmul(bank[6], lhsT=ones_c, rhs=cbf[:, j * TU:(j + 1) * TU], start=(j == 0), stop=(j == 3))],
                             reads=[B_cbf[j], B_cst], writes=[B_bank[6]])
                sch.pe_group([lambda e, j=j: e.matmul(bank[7], lhsT=ones_c, rhs=csq[:, j * TU:(j + 1) * TU], start=(j == 0), stop=(j == 3))],
                             reads=[B_csq[j], B_cst], writes=[B_bank[7]])

            for r in range(NR):
                for j in range(4):
                    b = next_bank()
                    pairs = []
                    for k in range(31):
                        if _DBG.get("evenk") and k % 2:
                            continue
                        c0 = j * UW + r * TU + 2 + k
                        pairs.append((diag[:, (j * 31 + k) * 128:(j * 31 + k + 1) * 128], upad[:, c0:c0 + TU]))
                    rd = [B_diag[j], B_u[j][r]] + ([B_u[j][r - 1]] if r > 0 else [B_upad0[j]])
                    MM(b, pairs, rd)
                    ACT(c32[:, j * TU:(j + 1) * TU], bank[b], AF.Identity, [B_bank[b], B_vec], [B_c32[j]], bias=vcol(C_DWB + j))
                    ACT(csq[:, j * TU:(j + 1) * TU], bank[b], AF.Square, [B_bank[b], B_vec], [B_csq[j]], bias=vcol(C_DWB + j))
                    sch.op("dve", lambda e, j=j: e.tensor_copy(out=cbf[:, j * TU:(j + 1) * TU], in_=c32[:, j * TU:(j + 1) * TU]),
                           reads=[B_c32[j]], writes=[B_cbf[j]])
                    if j > 0:
                        conv_stats(j - 1)
                conv_stats(3)
                ACT(mean_sb, bank[6], AF.Identity, [B_bank[6]], [B_mean])
                STT(sqv2, mean_sb, -1.0, mean_sb, ALU.mult, ALU.mult, [B_mean], [B_sqv2])
                TT(sqv2, bank[7], sqv2, ALU.add, [B_bank[7], B_sqv2], [B_sqv2])
                ACT(sqv2, sqv2, AF.Ln, [B_sqv2, B_vec], [B_sqv2], bias=vcol(C_EPSL))
                ACT(rstd2, sqv2, AF.Exp, [B_sqv2], [B_rstd2], scale=-0.5)
                for j in range(4):
                    i = j % 2
                    TT(t1[i], c32[:, j * TU:(j + 1) * TU], mean_sb, ALU.subtract, [B_c32[j], B_mean], [B_t1[i]])
                    TT(t1[i], t1[i], rstd2, ALU.mult, [B_t1[i], B_rstd2], [B_t1[i]])
                    ACT(sbuf_s[:, j * S + r * TU: j * S + (r + 1) * TU], t1[i], AF.Silu, [B_t1[i], B_vec], [B_s[j][r]],
                        bias=vcol(C_LNB + j), scale=vcol(C_LNG + j))

            checkpoint(3)
            O_H1, O_H2 = 61568, 109824
            cactH = f32ap(M + O_H1, 512); B_cactH = mbuf("cactH", O_H1, 2048)
            ringG = [f32ap(M + O_H1 + 2048, 256), f32ap(M + O_H2, 256), f32ap(M + O_H2 + 1024, 256)]
            B_ringG = [mbuf("ringG0", O_H1 + 2048, 1024), mbuf("ringG1", O_H2, 1024), mbuf("ringG2", O_H2 + 1024, 1024)]
            ringG_sem = [dsem("d_ringG%d" % i) for i in range(3)]
            modq = f32ap(M + O_H2 + 2048, 96); B_modq = mbuf("modq", O_H2 + 2048, 384)
            s_cH = dsem("d_cactH")
            bgp = [(h, n, q) for h in (0, 1) for n in range(16, 40) for q in (0, 1)]
            bg = {"dma": 0, "stt": 0, "init": False, "fin": False}

            def bg_half(h):
                sch.dma("sp", s_cH, lambda e, h=h: e.dma_start(out=cactH, in_=cb_d[:, h * 512:(h + 1) * 512]), writes=[B_cactH])
                ACT(cactH, cactH, AF.Silu, [B_cactH], [B_cactH])

            def bg_dma():
                c = bg["dma"]
                if c >= len(bgp) or c - bg["stt"] >= 3:
                    return
                h, n, q = bgp[c]
                if c == 48:
                    sch.dma("sp", s_cH, lambda e: e.dma_start(out=cactH, in_=cb_d[:, 512:1024]), writes=[B_cactH])
                    bg["half1_step"] = bg.get("step", 0)
                    bg["silu_pending"] = True
                bg["dma"] += 1
                i = c % 3
                off = n * 1024 + h * 512 + q * 256
                sch.dma("sp", ringG_sem[i], lambda e, i=i, off=off: e.dma_start(out=ringG[i], in_=wada_d[:, off:off + 256]),
                        writes=[B_ringG[i]])

            def bg_step():
                if bg["fin"]:
                    return
                bg["step"] = bg.get("step", 0) + 1
                if not bg["init"]:
                    bg["init"] = True
                    bg_half(0)
                    sch.op("dve", lambda e: e.memset(modq, 0.0), writes=[B_modq])
                    for _ in range(3):
                        bg_dma()
                    return
                if bg.get("silu_pending") and bg["step"] >= bg["half1_step"] + 1:
                    ACT(cactH, cactH, AF.Silu, [B_cactH], [B_cactH])
                    bg["silu_pending"] = False
                    bg["silu_step"] = bg["step"]
                n_stt = min(3, bg["dma"] - bg["stt"])
                if bg["stt"] >= 48 and (bg.get("silu_pending") or bg["step"] < bg.get("silu_step", 0) + 2):
                    n_stt = 0
                for _ in range(n_stt):
                    c = bg["stt"]
                    bg["stt"] += 1
                    h, n, q = bgp[c]
                    i = c % 3
                    idx = (h * 24 + (n - 16)) * 2 + q
                    STT(bank[7][:, 0:256], ringG[i], 1.0, cactH[:, q * 256:(q + 1) * 256], ALU.mult, ALU.mult,
                        [B_ringG[i], B_cactH], [B_bank[7], B_modq], accum_out=modq[:, idx:idx + 1])
                for _ in range(3):
                    bg_dma()
                if bg["stt"] >= len(bgp):
                    bg["fin"] = True
                    mq = modq.rearrange("p (h n q) -> p h n q", h=2, q=2)
                    TT(mq[:, :, :, 0], mq[:, :, :, 0], mq[:, :, :, 1], ALU.add, [B_modq], [B_modq])
                    TT(mod[:, 16:40], mq[:, 0, :, 0], mq[:, 1, :, 0], ALU.add, [B_modq], [B_mod])
                    ada_finish(16, 40)
                    sch.op("dve", lambda e: e.scalar_tensor_tensor(out=gm[:, 8:16], in0=mod[:, 32:40], scalar=1.0,
                                                                   in1=vecs[:, C_G2:C_G2 + 8], op0=ALU.add, op1=ALU.mult),
                           reads=[B_mod, B_vec], writes=[B_gm])

            for i in range(2):
                sch.op("pool", lambda e, i=i: e.memset(cvp[i][:, 0:2], 0.0), writes=[B_cvp0[i]])
            load_slot("E", w8_d[4])
            load_slot("SWO", w4_d[1])
            load_slot("CWO", w4_d[0])
            sc_slots = ["A", "B", "E"]
            cnt = 0
            for j in range(4):
                cv = cvp[j % 2]
                for r in range(NR):
                    units = []
                    for q in range(3):
                        seq = j * 3 + q
                        units.append((sc_slots[seq // 4], seq % 4))
                    i = cnt % 2
                    cnt += 1
                    b = next_bank()
                    MM(b, w8pairs(units[0][0], units[0][1], r), h_reads(r) + [slot_buf[units[0][0]]])
                    ACT(gcs[i], bank[b], AF.Identity, [B_bank[b]], [B_gcs[i]])
                    b = next_bank()
                    MM(b, w8pairs(units[1][0], units[1][1], r), h_reads(r) + [slot_buf[units[1][0]]])
                    TT(cv[:, 2 + r * TU: 2 + (r + 1) * TU], bank[b], gcs[i], ALU.mult, [B_bank[b], B_gcs[i]], [B_cvp[j % 2][r]])
                    rd = [B_cvp[j % 2][r]] + ([B_cvp[j % 2][r - 1]] if r > 0 else [B_cvp0[j % 2]])
                    TS(acc[i], cv[:, r * TU + 2: r * TU + 2 + TU], vcol(C_SCW + j * 3 + 2), ALU.mult, rd + [B_vec], [B_acc[i]])
                    STT(acc[i], cv[:, r * TU + 1: r * TU + 1 + TU], vcol(C_SCW + j * 3 + 1), acc[i], ALU.mult, ALU.add,
                        rd + [B_vec, B_acc[i]], [B_acc[i]])
                    STT(acc[i], cv[:, r * TU: r * TU + TU], vcol(C_SCW + j * 3), acc[i], ALU.mult, ALU.add,
                        rd + [B_vec, B_acc[i]], [B_acc[i]])
                    b = next_bank()
                    MM(b, w8pairs(units[2][0], units[2][1], r), h_reads(r) + [slot_buf[units[2][0]]])
                    TT(vbuf[:, j * S + r * TU: j * S + (r + 1) * TU], bank[b], acc[i], ALU.mult, [B_bank[b], B_acc[i]], [B_v[j][r]])
                    if j >= 2:
                        bg_step()
                if j == 1:
                    load_slot("A", w8_d[5])
                if j == 2:
                    load_slot("B", w8_d[6])

            checkpoint(4)
            gate_slots = ["A", "B", "A", "B"]
            for m in range(KD):
                gi = m // 2
                slot = gate_slots[gi]
                blkA, blkB = (m % 2) * 2, (m % 2) * 2 + 1
                for r in range(NR):
                    b = next_bank()
                    MM(b, w8pairs(slot, blkA, r), h_reads(r) + [slot_buf[slot]])
                    ACT(gsb[0], bank[b], AF.Sigmoid, [B_bank[b], B_vec], [B_gsb[0]], bias=vcol(C_BGATE + m))
                    b = next_bank()
                    MM(b, [(slot_ap["CWO"][:, kt * 1024 + m * 128: kt * 1024 + (m + 1) * 128], sbuf_s[:, kt * S + r * TU: kt * S + (r + 1) * TU])
                           for kt in range(4)], [B_s[kt][r] for kt in range(4)] + [slot_buf["CWO"]])
                    i = r % 2
                    STT(gy[i], bank[b], vcol(C_CBO + m), gsb[0], ALU.add, ALU.mult, [B_bank[b], B_gsb[0], B_vec], [B_gy[i]])
                    b = next_bank()
                    MM(b, w8pairs(slot, blkB, r), h_reads(r) + [slot_buf[slot]])
                    ACT(gsb[1], bank[b], AF.Sigmoid, [B_bank[b], B_vec], [B_gsb[1]], bias=vcol(C_BGATE + 8 + m))
                    b = next_bank()
                    MM(b, [(slot_ap["SWO"][:, kt * 1024 + m * 128: kt * 1024 + (m + 1) * 128], vbuf[:, kt * S + r * TU: kt * S + (r + 1) * TU])
                           for kt in range(4)], [B_v[kt][r] for kt in range(4)] + [slot_buf["SWO"]])
                    TT(gsb[1], bank[b], gsb[1], ALU.mult, [B_bank[b], B_gsb[1]], [B_gsb[1]])
                    TT(mixed[:, m * S + r * TU: m * S + (r + 1) * TU], gsb[1], gy[i], ALU.add, [B_gsb[1], B_gy[i]], [B_mix[m][r]])
                    bg_step()
                if m == 1:
                    load_slot("A", w8_d[7])
                if m == 3:
                    load_slot("B", w8_d[8])
                if m == 5:
                    load_slot("A", w8_d[9])
            for _ in range(8):
                bg_step()
            assert bg["fin"]
            load_slot("B", w8_d[10])
            load_slot("C", w8_d[11])
            load_slot("D", w8_d[12])

            checkpoint(5)
            def wo_units(r):
                for m in range(KD):
                    slot = "A" if m < 4 else "B"
                    b = next_bank()
                    MM(b, [(slot_ap[slot][:, kt * 512 + (m % 4) * 128: kt * 512 + (m % 4 + 1) * 128],
                            mixed[:, kt * S + r * TU: kt * S + (r + 1) * TU]) for kt in range(KD)],
                       [B_mix[kt][r] for kt in range(KD)] + [slot_buf[slot]])
                    xk = xs[:, m * S + r * TU: m * S + (r + 1) * TU]
                    STT(xk, bank[b], mod[:, 16 + m:17 + m], xk, ALU.mult, ALU.add, [B_bank[b], B_mod, B_xs[m][r]], [B_xs[m][r]])

            ffn_slots = ["C", "D", "A", "B"]
            fs = {"i": 0, "next_gu": 13, "next_wd": 0}
            gu_slot = {}

            O_CF = 102400
            cactF = f32ap(M + O_CF, 1024); B_cactF = mbuf("cactF", O_CF, 4096)
            ringF = [f32ap(M + O_CF + 4096 + i * 2048, 512) for i in range(2)]
            B_ringF = [mbuf("ringF%d" % i, O_CF + 4096 + i * 2048, 2048) for i in range(2)]
            ringF_sem = [dsem("d_ringF%d" % i) for i in range(2)]
            modh2 = f32ap(M + O_CF + 8192, 16); B_modh2 = mbuf("modh2", O_CF + 8192, 64)
            s_cbF = dsem("d_cbF")
            gt2 = {"dma": 0, "stt": 0, "init": False, "fin": False}

            def gt2_dma():
                c = gt2["dma"]
                if c >= 16:
                    return
                gt2["dma"] += 1
                n, h, i = 40 + c // 2, c % 2, c % 2
                sch.dma("sp", ringF_sem[i], lambda e, i=i, n=n, h=h: e.dma_start(
                    out=ringF[i], in_=wada_d[:, n * 1024 + h * 512: n * 1024 + (h + 1) * 512]), writes=[B_ringF[i]])

            def gt2_step():
                if not gt2["init"]:
                    gt2["init"] = True
                    sch.dma("sp", s_cbF, lambda e: e.dma_start(out=cactF, in_=cb_d), writes=[B_cactF])
                    ACT(cactF, cactF, AF.Silu, [B_cactF], [B_cactF])
                    sch.op("dve", lambda e: e.memset(modh2, 0.0), writes=[B_modh2])
                    gt2_dma()
                    gt2_dma()
                    gt2["tick"] = 0
                    return
                gt2["tick"] += 1
                if gt2["tick"] < 3 or gt2["tick"] % 2 == 0:
                    return
                c = gt2["stt"]
                if c < 16:
                    gt2["stt"] += 1
                    h, i = c % 2, c % 2
                    STT(bank[7], ringF[i], 1.0, cactF[:, h * 512:(h + 1) * 512], ALU.mult, ALU.mult,
                        [B_ringF[i], B_cactF], [B_bank[7], B_modh2], accum_out=modh2[:, c:c + 1])
                    gt2_dma()
                elif not gt2["fin"]:
                    gt2["fin"] = True
                    mh = modh2.rearrange("p (n h) -> p n h", h=2)
                    TT(mod[:, 40:48], mh[:, :, 0], mh[:, :, 1], ALU.add, [B_modh2], [B_mod])
                    ada_finish(40, 48)

            gu_done = set()
            fcnt = {"i": 0}

            def gu_unit(g, j, r):
                j0, j1 = FGROUPS[g]
                q = j // 2
                if q not in gu_slot:
                    slot = ffn_slots[fs["i"] % 4]
                    fs["i"] += 1
                    if q >= 2:
                        load_slot(slot, w8_d[11 + q])
                    gu_slot[q] = slot
                slot = gu_slot[q]
                blk = (j % 2) * 2
                i = fcnt["i"] % 2
                fcnt["i"] += 1
                b = next_bank()
                MM(b, w8pairs(slot, blk, r), h_reads(r) + [slot_buf[slot]])
                ACT(sg[i], bank[b], AF.Silu, [B_bank[b]], [B_sg[i]])
                b = next_bank()
                MM(b, w8pairs(slot, blk + 1, r), h_reads(r) + [slot_buf[slot]])
                jj = j - j0
                TT(abuf[g % 2][:, jj * S + r * TU: jj * S + (r + 1) * TU], bank[b], sg[i], ALU.mult,
                   [B_bank[b], B_sg[i]], [B_ab[g % 2][jj][r]])
                gt2_step()
                gu_done.add((j, r))

            def fill0():
                for j in (0, 1):
                    gu_unit(0, j, 0)

            def fill1(after_unit):
                for (j, rr) in [(0, 1), (1, 1), (2, 0), (3, 0), (2, 1), (3, 1), (0, 2), (1, 2)]:
                    gu_unit(0, j, rr)
                    after_unit()

            norm_pipeline(wo_units, 2, fillers=[fill0, fill1])

            checkpoint(6)
            def gu_stage(g):
                j0, j1 = FGROUPS[g]
                if g == 0:
                    order = [(j, r) for r in range(NR) for j in (0, 1)] + [(j, r) for j in range(2, j1) for r in range(NR)]
                else:
                    order = [(j, r) for j in range(j0, j1) for r in range(NR)]
                for (j, r) in order:
                    if (j, r) not in gu_done:
                        gu_unit(g, j, r)

            def down_stage(g, last):
                j0, j1 = FGROUPS[g]
                nk = j1 - j0
                slots = []
                for half in range(2):
                    slot = ffn_slots[fs["i"] % 4]
                    fs["i"] += 1
                    load_slot(slot, wd_d[g * 2 + half], ncols=3072)
                    slots.append(slot)

                def unit(m, r):
                    slot = slots[m // 4]
                    b = next_bank()
                    MM(b, [(slot_ap[slot][:, kk * 512 + (m % 4) * 128: kk * 512 + (m % 4 + 1) * 128],
                            abuf[g % 2][:, kk * S + r * TU: kk * S + (r + 1) * TU]) for kk in range(nk)],
                       [B_ab[g % 2][kk][r] for kk in range(nk)] + [slot_buf[slot]])
                    xk = xs[:, m * S + r * TU: m * S + (r + 1) * TU]
                    STT(xk, bank[b], mod[:, 40 + m:41 + m], xk, ALU.mult, ALU.add, [B_bank[b], B_mod, B_xs[m][r]], [B_xs[m][r]])

                if not last:
                    for m in range(KD):
                        for r in range(NR):
                            unit(m, r)
                else:
                    def last_units(r):
                        for m in range(KD):
                            unit(m, r)
                    norm_pipeline(last_units, "final")

            gu_stage(0)
            gu_stage(1)
            for _ in range(80):
                if gt2["fin"]:
                    break
                gt2_step()
            assert gt2["fin"]
            down_stage(0, False)
            gu_stage(2)
            down_stage(1, False)
            gu_stage(3)
            down_stage(2, False)
            down_stage(3, True)


        try:
            stages()
        except _Stop:
            for k in range(KD):
                for r in range(NR):
                    i = (k * NR + r) % 2
                    sch.dma("sp", otile_sem[i], lambda e, k=k, r=r: e.dma_start(
                        out=outT3[:, k, R(r)], in_=xs[:, k * S + r * TU: k * S + (r + 1) * TU]), reads=[B_xs[k][r]])

        for i in range(NOT):
            if sch.dma_cnt.get(otile_sem[i], 0):
                sch.wait_tok("sp", (otile_sem[i], sch.dma_cnt[otile_sem[i]]))
        sch.emit()
    return nc


def _chunk(v):
    return np.ascontiguousarray(np.asarray(v, np.float32).reshape(-1, 128).T)


def _unit(W, col_starts, kpad=None):
    KT = W.shape[0] // 128
    Wr = W.reshape(KT, 128, W.shape[1])
    U = np.concatenate([Wr[:, :, c:c + 128] for c in col_starts], axis=2)
    U = U.transpose(1, 0, 2)
    if kpad is not None and kpad > KT:
        U = np.concatenate([U, np.zeros((128, kpad - KT, U.shape[2]), np.float32)], axis=1)
    return np.ascontiguousarray(U).reshape(128, -1)


_NC_CACHE = {}


def _prep(x, c, w_ada, b_ada, norm1_g, w_in, b_glu, conf_dw_w, conf_dw_b, conf_ln_g, conf_ln_b, conf_w_out,
          conf_b_out, sc_dw_w, sc_w_out, b_gate, w_o, norm2_g, w_gu, w_down, final_g, cores=None):
    f = lambda a: np.asarray(a, np.float32)
    x, c = f(x), f(c)
    w_ada, w_in, w_o, w_gu, w_down = f(w_ada)[0], f(w_in)[0], f(w_o)[0], f(w_gu)[0], f(w_down)[0]
    conf_w_out, sc_w_out = f(conf_w_out)[0], f(sc_w_out)[0]

    vecs = np.zeros((128, NV), np.float32)
    vecs[:, C_BADA:C_BADA + 48] = _chunk(f(b_ada)[0])
    vecs[:, C_G1:C_G1 + 8] = _chunk(f(norm1_g)[0])
    vecs[:, C_BGLU:C_BGLU + 8] = _chunk(f(b_glu)[0])
    vecs[:, C_DWB:C_DWB + 4] = _chunk(f(conf_dw_b)[0])
    vecs[:, C_LNG:C_LNG + 4] = _chunk(f(conf_ln_g)[0])
    vecs[:, C_LNB:C_LNB + 4] = _chunk(f(conf_ln_b)[0])
    vecs[:, C_CBO:C_CBO + 8] = _chunk(f(conf_b_out)[0])
    vecs[:, C_BGATE:C_BGATE + 16] = _chunk(f(b_gate)[0])
    vecs[:, C_G2:C_G2 + 8] = _chunk(f(norm2_g)[0])
    vecs[:, C_GF:C_GF + 8] = _chunk(f(final_g))
    dww = f(conf_dw_w)[0]
    vecs[:, C_DWW:C_DWW + 124] = dww.reshape(31, 4, 128).transpose(2, 1, 0).reshape(128, 124)
    scw = f(sc_dw_w)[0]
    vecs[:, C_SCW:C_SCW + 12] = scw.reshape(3, 4, 128).transpose(2, 1, 0).reshape(128, 12)
    vecs[:, C_EPSR] = 1e-6
    vecs[:, C_EPSL] = 1e-5

    cst = np.zeros((128, 384), np.float32)
    cst[:, 0:128] = np.eye(128, dtype=np.float32)
    cst[:, 128:256] = 1.0 / 1024.0
    cst[:, 256:384] = 1.0 / 512.0

    wada = np.ascontiguousarray(w_ada.reshape(1024, 48, 128).transpose(2, 1, 0)).reshape(128, 48 * 1024)

    ua = lambda j: 128 * j
    ub = lambda j: 512 + 128 * j
    sb = lambda j: 1024 + 128 * j
    gc = lambda j: 1536 + 128 * j
    vv = lambda j: 2048 + 128 * j
    gA = lambda m: 2560 + 128 * m
    gB = lambda m: 3584 + 128 * m
    units = []
    units.append(_unit(w_in, [ub(0), ua(0), ub(1), ua(1)]))
    units.append(_unit(w_in, [ub(2), ua(2), ub(3), ua(3)]))
    seq = []
    for j in range(4):
        seq += [gc(j), vv(j), sb(j)]
    for i in range(3):
        units.append(_unit(w_in, seq[i * 4:(i + 1) * 4]))
    for i in range(4):
        units.append(_unit(w_in, [gA(2 * i), gB(2 * i), gA(2 * i + 1), gB(2 * i + 1)]))
    units.append(_unit(w_o, [0, 128, 256, 384]))
    units.append(_unit(w_o, [512, 640, 768, 896]))
    for q in range(11):
        units.append(_unit(w_gu, [128 * (2 * q), 2816 + 128 * (2 * q), 128 * (2 * q + 1), 2816 + 128 * (2 * q + 1)]))
    w8 = np.stack(units, axis=0)
    w4 = np.stack([_unit(conf_w_out, [128 * m for m in range(8)]), _unit(sc_w_out, [128 * m for m in range(8)])], axis=0)
    wdu = []
    for (j0, j1) in FGROUPS:
        for half in range(2):
            wdu.append(_unit(w_down[j0 * 128:j1 * 128], [half * 512 + 128 * mm for mm in range(4)], kpad=6))
    wd = np.stack(wdu, axis=0)

    in_maps = []
    for b in (range(NCORES) if cores is None else cores):
        in_maps.append({
            "xT": np.ascontiguousarray(x[b].T),
            "cb": np.ascontiguousarray(np.broadcast_to(c[b][None, :], (128, 1024))),
            "wada": wada, "vecs": vecs, "cst": cst, "w8": w8, "w4": w4, "wd": wd,
        })
    return in_maps


def kernel(**inputs):
    in_maps = _prep(**inputs)
    if "nc" not in _NC_CACHE:
        _NC_CACHE["nc"] = build_nc()
    nc = _NC_CACHE["nc"]
    res = run_bass_kernel_spmd(nc, in_maps, core_ids=list(range(NCORES)))
    out = np.stack([np.ascontiguousarray(res.results[b]["outT"].T) for b in range(NCORES)], axis=0)
    return out.astype(np.float32)
```

```python
from contextlib import ExitStack
import numpy as np
import concourse.bass as bass
import concourse.mybir as mybir
from concourse.bass_utils import run_bass_kernel_spmd

F32 = mybir.dt.float32
BF16 = mybir.dt.bfloat16
ALU = mybir.AluOpType
AF = mybir.ActivationFunctionType

ENGS = ("pe", "act", "dve", "pool", "sp")
NCORES = 8
S = 2048
TU = 512
NR = 4
KD = 8
KF = 22
NV = 256
FGROUPS = [(0, 4), (4, 10), (10, 16), (16, 22)]

C_BADA, C_G1, C_BGLU, C_DWB, C_LNG, C_LNB, C_CBO, C_BGATE, C_G2, C_GF, C_DWW, C_SCW, C_EPSR, C_EPSL = (
    0, 48, 56, 64, 68, 72, 76, 84, 100, 108, 116, 240, 252, 253)


class Buf:
    __slots__ = ("name", "w", "r", "lo", "hi", "ov", "excl")

    def __init__(self, name, lo=None, hi=None):
        self.name = name
        self.excl = False
        self.w = None
        self.r = {}
        self.lo = lo
        self.hi = hi
        self.ov = []


class Sched:
    def __init__(self, nc):
        self.nc = nc
        self.ops = {e: [] for e in ENGS}
        self.cnt = {e: 0 for e in ENGS}
        self.known = {e: {} for e in ENGS}
        self.sems = {}
        self.dma_cnt = {}
        self.ibufs = []

    def add_sem(self, key, handle):
        self.sems[key] = handle

    def buf(self, name, lo=None, hi=None):
        b = Buf(name, lo, hi)
        if lo is not None:
            for o in self.ibufs:
                if o.lo < hi and lo < o.hi:
                    o.ov.append(b)
                    b.ov.append(o)
            self.ibufs.append(b)
        return b

    def _need(self, eng, tok):
        if tok is None:
            return
        key, val = tok
        if key == "pe" and eng == "pe":
            return
        if self.known[eng].get(key, 0) >= val:
            return
        self.known[eng][key] = val
        h = self.sems[key]
        self.ops[eng].append(lambda e, h=h, val=val: e.wait_ge(h, val))

    def deps(self, eng, reads, writes):
        for b in reads:
            self._need(eng, b.w)
            for o in b.ov:
                self._need(eng, o.w)
            if b.excl:
                for k, v in b.r.items():
                    if k != eng:
                        self._need(eng, (k, v))
        for b in writes:
            self._need(eng, b.w)
            for k, v in b.r.items():
                self._need(eng, (k, v))
            for o in b.ov:
                self._need(eng, o.w)
                for k, v in o.r.items():
                    self._need(eng, (k, v))

    def _mark(self, tok, reads, writes):
        for b in reads:
            if b.r.get(tok[0], 0) < tok[1]:
                b.r[tok[0]] = tok[1]
        for b in writes:
            b.w = tok
            b.r = {}

    def op(self, eng, fn, reads=(), writes=()):
        self.deps(eng, reads, writes)
        self.cnt[eng] += 1
        tok = (eng, self.cnt[eng])
        h = self.sems[eng]
        self.ops[eng].append(lambda e, fn=fn, h=h: fn(e).then_inc(h, 1))
        self._mark(tok, reads, writes)

    def pe_group(self, fns, reads=(), writes=()):
        self.deps("pe", reads, writes)
        self.cnt["pe"] += 1
        tok = ("pe", self.cnt["pe"])
        h = self.sems["pe"]
        for f in fns[:-1]:
            self.ops["pe"].append(lambda e, f=f: f(e))
        self.ops["pe"].append(lambda e, f=fns[-1], h=h: f(e).then_inc(h, 1))
        self._mark(tok, reads, writes)

    def dma(self, eng, semkey, fn, reads=(), writes=()):
        self.deps(eng, reads, writes)
        self.dma_cnt[semkey] = self.dma_cnt.get(semkey, 0) + 16
        tok = (semkey, self.dma_cnt[semkey])
        h = self.sems[semkey]
        self.ops[eng].append(lambda e, fn=fn, h=h: fn(e).then_inc(h, 16))
        self._mark(tok, reads, writes)
        return tok

    def wait_tok(self, eng, tok):
        self._need(eng, tok)

    def emit(self):
        with self.nc.Block() as block:
            @block.tensor
            def _(e):
                for f in self.ops["pe"]:
                    f(e)

            @block.scalar
            def _(e):
                for f in self.ops["act"]:
                    f(e)

            @block.vector
            def _(e):
                for f in self.ops["dve"]:
                    f(e)

            @block.gpsimd
            def _(e):
                for f in self.ops["pool"]:
                    f(e)

            @block.sync
            def _(e):
                for f in self.ops["sp"]:
                    f(e)


_DBG = {}


class _Stop(Exception):
    pass


def build_nc(stop=None):
    nc = bass.Bass("TRN2", target_bir_lowering=False)
    din = lambda n, s: nc.dram_tensor(n, s, F32, kind="ExternalInput").ap()
    xT = din("xT", [1024, S])
    cb_d = din("cb", [128, 1024])
    wada_d = din("wada", [128, 48 * 1024])
    vecs_d = din("vecs", [128, NV])
    cst_d = din("cst", [128, 384])
    w8_d = din("w8", [22, 128, 4096])
    w4_d = din("w4", [2, 128, 4096])
    wd_d = din("wd", [8, 128, 3072])
    out_d = nc.dram_tensor("outT", [1024, S], F32, kind="ExternalOutput").ap()

    with ExitStack() as st:
        NARENA = 53200
        arena = st.enter_context(nc.sbuf_tensor("arena", [128, NARENA], F32))
        ps = st.enter_context(nc.psum_tensor("ps", [128, 8 * TU], F32))
        sch = Sched(nc)
        for e in ENGS:
            sch.add_sem(e, st.enter_context(nc.semaphore("s_" + e)))

        def dsem(name):
            sch.add_sem(name, st.enter_context(nc.semaphore(name)))
            return name

        def f32ap(off, n):
            assert off % 4 == 0 and off + 4 * n <= NARENA * 4, (off, n)
            return arena[:, off // 4: off // 4 + n]

        def bfap(off, n):
            assert off % 4 == 0 and n % 2 == 0 and off + 2 * n <= NARENA * 4, (off, n)
            return arena[:, off // 4: off // 4 + n // 2].bitcast(BF16)

        O_XS, O_H, O_VEC, O_CST, O_MOD, O_GM = 0, 65536, 98304, 99328, 100096, 100288
        M = 100352
        xs = f32ap(O_XS, KD * S)
        hbuf = bfap(O_H, KD * S)
        vecs = f32ap(O_VEC, NV)
        cst = bfap(O_CST, 384)
        mod = f32ap(O_MOD, 48)
        gm = f32ap(O_GM, 16)
        ident, ones_d, ones_c = cst[:, 0:128], cst[:, 128:256], cst[:, 256:384]

        B_xs = [[sch.buf("xs%d_%d" % (k, r), O_XS + (k * S + r * TU) * 4, O_XS + (k * S + (r + 1) * TU) * 4)
                 for r in range(NR)] for k in range(KD)]
        B_h = [[sch.buf("h%d_%d" % (k, r), O_H + (k * S + r * TU) * 2, O_H + (k * S + (r + 1) * TU) * 2)
                for r in range(NR)] for k in range(KD)]
        B_vec = sch.buf("vecs", O_VEC, O_VEC + NV * 4)
        B_cst = sch.buf("cst", O_CST, O_CST + 768)
        B_mod = sch.buf("mod", O_MOD, O_MOD + 192)
        B_gm = sch.buf("gm", O_GM, O_GM + 64)

        def mbuf(name, off, nbytes):
            return sch.buf(name, M + off, M + off + nbytes)

        SLOT_OFF = {"A": 0, "B": 8192, "C": 65536, "D": 73728, "CWO": 97536, "SWO": 49280, "E": 41088}
        slot_ap = {k: bfap(M + v, 4096) for k, v in SLOT_OFF.items()}
        slot_buf = {k: mbuf("slot" + k, v, 8192) for k, v in SLOT_OFF.items()}
        slot_sem = {k: dsem("d_slot" + k) for k in SLOT_OFF}

        O_DIAG, O_U, O_S = 16384, 48128, 64768
        UW = 32 + S
        diag = bfap(M + O_DIAG, 4 * 31 * 128)
        B_diag = [mbuf("diag%d" % j, O_DIAG + j * 31 * 256, 31 * 256) for j in range(4)]
        upad = bfap(M + O_U, 4 * UW)
        B_upad0 = [mbuf("upad%d" % j, O_U + j * UW * 2, 64) for j in range(4)]
        B_u = [[mbuf("u%d_%d" % (j, r), O_U + (j * UW + 32 + r * TU) * 2, TU * 2) for r in range(NR)] for j in range(4)]
        sbuf_s = bfap(M + O_S, 4 * S)
        B_s = [[mbuf("s%d_%d" % (j, r), O_S + (j * S + r * TU) * 2, TU * 2) for r in range(NR)] for j in range(4)]
        O_C32, O_CBF, O_CSQ, O_MEAN, O_SQV2, O_RSTD2, O_T1, O_SIG = 81152, 89344, 93440, 97536, 99584, 101632, 103680, 107776
        c32 = f32ap(M + O_C32, 4 * TU); B_c32 = [mbuf("c32_%d" % j, O_C32 + j * 2048, 2048) for j in range(4)]
        cbf = bfap(M + O_CBF, 4 * TU); B_cbf = [mbuf("cbf%d" % j, O_CBF + j * 1024, 1024) for j in range(4)]
        csq = bfap(M + O_CSQ, 4 * TU); B_csq = [mbuf("csq%d" % j, O_CSQ + j * 1024, 1024) for j in range(4)]
        mean_sb = f32ap(M + O_MEAN, TU); B_mean = mbuf("mean", O_MEAN, 2048)
        sqv2 = f32ap(M + O_SQV2, TU); B_sqv2 = mbuf("sqv2", O_SQV2, 2048)
        rstd2 = f32ap(M + O_RSTD2, TU); B_rstd2 = mbuf("rstd2", O_RSTD2, 2048)
        t1 = [f32ap(M + O_T1 + i * 2048, TU) for i in range(2)]; B_t1 = [mbuf("t1_%d" % i, O_T1 + i * 2048, 2048) for i in range(2)]
        sig = [f32ap(M + O_SIG + i * 2048, TU) for i in range(2)]; B_sig = [mbuf("sig%d" % i, O_SIG + i * 2048, 2048) for i in range(2)]
        O_WADA, O_CACT, O_JUNK, O_CB = 64768, 97536, 101632, 105728
        NWR = 4
        WADA_OFF = [O_WADA, O_WADA + 8192, O_DIAG, O_DIAG + 8192]
        wada = [f32ap(M + WADA_OFF[i], 2048) for i in range(NWR)]
        B_wada = [mbuf("wada%d" % i, WADA_OFF[i], 8192) for i in range(NWR)]
        wada_sem = [dsem("d_wada%d" % i) for i in range(NWR)]
        cact = f32ap(M + O_CACT, 1024); B_cact = mbuf("cact", O_CACT, 4096)
        junk = f32ap(M + O_JUNK, 1024); B_junk = mbuf("junk", O_JUNK, 4096)
        cbt = f32ap(M + O_CB, 1024); B_cb = mbuf("cb", O_CB, 4096)
        class NT:
            pass

        def norm_temps(base, sqbase, tag):
            t = NT()
            t.sq = [bfap(M + sqbase + i * 1024, TU) for i in range(8)]
            t.B_sq = [mbuf(tag + "sq%d" % i, sqbase + i * 1024, 1024) for i in range(8)]
            t.rstd = [f32ap(M + base + i * 2048, TU) for i in range(2)]
            t.B_rstd = [mbuf(tag + "rstd%d" % i, base + i * 2048, 2048) for i in range(2)]
            t.tmp = [f32ap(M + base + 4096 + i * 2048, TU) for i in range(2)]
            t.B_tmp = [mbuf(tag + "tmp%d" % i, base + 4096 + i * 2048, 2048) for i in range(2)]
            return t

        NT_U = norm_temps(81152, 89344, "u")
        NT_N = norm_temps(94208, 86016, "n")
        NOT = 8
        ot_i = {"i": 0}
        otile_sem = [dsem("d_ot%d" % i) for i in range(NOT)]
        O_OT = 16384
        otile = [f32ap(M + O_OT + i * 2048, TU) for i in range(NOT)]; B_ot = [mbuf("ot%d" % i, O_OT + i * 2048, 2048) for i in range(NOT)]
        O_CVP, O_GCS, O_ACC, O_V = 16384, 32896, 36992, 81152
        CW = 2 + S
        cvp = [f32ap(M + O_CVP + i * 8256, CW) for i in range(2)]
        B_cvp0 = [mbuf("cvp0_%d" % i, O_CVP + i * 8256, 8) for i in range(2)]
        B_cvp = [[mbuf("cvp%d_%d" % (i, r), O_CVP + i * 8256 + (2 + r * TU) * 4, TU * 4) for r in range(NR)] for i in range(2)]
        gcs = [f32ap(M + O_GCS + i * 2048, TU) for i in range(2)]; B_gcs = [mbuf("gcs%d" % i, O_GCS + i * 2048, 2048) for i in range(2)]
        acc = [f32ap(M + O_ACC + i * 2048, TU) for i in range(2)]; B_acc = [mbuf("acc%d" % i, O_ACC + i * 2048, 2048) for i in range(2)]
        vbuf = bfap(M + O_V, 4 * S)
        B_v = [[mbuf("v%d_%d" % (j, r), O_V + (j * S + r * TU) * 2, TU * 2) for r in range(NR)] for j in range(4)]
        O_MIX, O_GSB, O_GY = 16384, 57472, 105728
        mixed = bfap(M + O_MIX, KD * S)
        B_mix = [[mbuf("mix%d_%d" % (m, r), O_MIX + (m * S + r * TU) * 2, TU * 2) for r in range(NR)] for m in range(KD)]
        gsb = [f32ap(M + O_GSB + i * 2048, TU) for i in range(2)]; B_gsb = [mbuf("gsb%d" % i, O_GSB + i * 2048, 2048) for i in range(2)]
        gy = [f32ap(M + O_GY + i * 2048, TU) for i in range(2)]; B_gy = [mbuf("gy%d" % i, O_GY + i * 2048, 2048) for i in range(2)]
        O_AB, O_SG = 16384, 81920
        abuf = [bfap(M + O_AB + g * 6 * 4096, 6 * S) for g in range(2)]
        B_ab = [[[mbuf("ab%d_%d_%d" % (g, jj, r), O_AB + g * 24576 + (jj * S + r * TU) * 2, TU * 2) for r in range(NR)]
                 for jj in range(6)] for g in range(2)]
        sg = [f32ap(M + O_SG + i * 2048, TU) for i in range(2)]; B_sg = [mbuf("sg%d" % i, O_SG + i * 2048, 2048) for i in range(2)]

        bank = [ps[:, b * TU:(b + 1) * TU] for b in range(8)]
        B_bank = [sch.buf("bank%d" % b) for b in range(8)]
        for bb in B_bank:
            bb.excl = True
        ring = {"i": 0}

        def next_bank():
            b = ring["i"] % 6
            ring["i"] += 1
            return b

        def R(r):
            return slice(r * TU, (r + 1) * TU)

        def vcol(c):
            return vecs[:, c:c + 1]

        def ACT(out, in_, func, reads, writes, bias=None, scale=None):
            kw = {}
            if bias is not None:
                kw["bias"] = bias
            if scale is not None:
                kw["scale"] = scale
            sch.op("act", lambda e: e.activation(out=out, in_=in_, func=func, **kw), reads=reads, writes=writes)

        def TT(out, in0, in1, op, reads, writes, eng="dve"):
            sch.op(eng, lambda e: e.tensor_tensor(out=out, in0=in0, in1=in1, op=op), reads=reads, writes=writes)

        def STT(out, in0, scalar, in1, op0, op1, reads, writes, accum_out=None):
            if accum_out is None:
                sch.op("dve", lambda e: e.scalar_tensor_tensor(out=out, in0=in0, scalar=scalar, in1=in1, op0=op0, op1=op1),
                       reads=reads, writes=writes)
            else:
                sch.op("dve", lambda e: e.scalar_tensor_tensor(out=out, in0=in0, scalar=scalar, in1=in1, op0=op0, op1=op1,
                                                               accum_out=accum_out), reads=reads, writes=writes)

        def TS(out, in0, scalar1, op0, reads, writes):
            sch.op("dve", lambda e: e.tensor_scalar(out=out, in0=in0, scalar1=scalar1, scalar2=None, op0=op0),
                   reads=reads, writes=writes)

        def MM(b, pairs, reads):
            fns = []
            n = len(pairs)
            for i, (lt, rh) in enumerate(pairs):
                fns.append(lambda e, lt=lt, rh=rh, i=i: e.matmul(bank[b], lhsT=lt, rhs=rh, start=(i == 0), stop=(i == n - 1)))
            sch.pe_group(fns, reads=reads, writes=[B_bank[b]])

        def load_slot(slot, src, ncols=4096):
            dst = slot_ap[slot][:, 0:ncols]
            return sch.dma("pool", slot_sem[slot], lambda e: e.dma_start(out=dst, in_=src), writes=[slot_buf[slot]])

        s_vec, s_cst, s_cb = dsem("d_vec"), dsem("d_cst"), dsem("d_cb")
        s_x = [dsem("d_x%d" % r) for r in range(NR)]
        sch.dma("sp", s_vec, lambda e: e.dma_start(out=vecs, in_=vecs_d), writes=[B_vec])
        sch.dma("sp", s_cb, lambda e: e.dma_start(out=cbt, in_=cb_d), writes=[B_cb])
        sch.dma("pool", s_cst, lambda e: e.dma_start(out=cst, in_=cst_d), writes=[B_cst])
        stage = [f32ap(O_H + i * 16384, 4096) for i in range(2)]
        B_stage = [sch.buf("stage%d" % i, O_H + i * 16384, O_H + (i + 1) * 16384) for i in range(2)]
        s_stage = [dsem("d_stage%d" % i) for i in range(2)]
        for i in range(2):
            sch.dma("sp", s_stage[i], lambda e, i=i: e.dma_start(out=stage[i], in_=w8_d[i]), writes=[B_stage[i]])
        xs3 = xs.rearrange("p (k t) -> p k t", k=KD)
        xT3 = xT.rearrange("(k p) t -> p k t", p=128)
        outT3 = out_d.rearrange("(k p) t -> p k t", p=128)

        def load_x(r, eng="sp"):
            sch.dma(eng, s_x[r], lambda e: e.dma_start(out=xs3[:, :, R(r)], in_=xT3[:, :, R(r)]),
                    writes=[B_xs[k][r] for k in range(KD)])

        load_x(0)

        def build_diag(j):
            dj = diag[:, j * 31 * 128:(j + 1) * 31 * 128]
            wj = vecs[:, C_DWW + j * 31: C_DWW + (j + 1) * 31]
            o3 = bass.AP(dj.tensor, dj.offset, [list(dj.ap[0]), [128, 31], [1, 128]])
            i3 = bass.AP(ident.tensor, ident.offset, [list(ident.ap[0]), [0, 31], [1, 128]])
            w3 = bass.AP(wj.tensor, wj.offset, [list(wj.ap[0]), [1, 31], [0, 128]])
            sch.op("dve", lambda e, o3=o3, i3=i3, w3=w3: e.tensor_tensor(out=o3, in0=i3, in1=w3, op=ALU.mult),
                   reads=[B_cst, B_vec], writes=[B_diag[j]])

        ACT(cact, cbt, AF.Silu, [B_cb], [B_cact])
        for i, slot in enumerate(("A", "B")):
            for hh in range(4):
                ACT(slot_ap[slot][:, hh * 1024:(hh + 1) * 1024], stage[i][:, hh * 1024:(hh + 1) * 1024], AF.Identity,
                    [B_stage[i]], [slot_buf[slot]])
        sch.op("dve", lambda e: e.memset(mod, 0.0), writes=[B_mod])
        wada_i = {"i": 0}
        ada_tok = {}

        def ada_group(g):
            i = wada_i["i"] % NWR
            wada_i["i"] += 1
            ada_tok[g] = sch.dma("sp", wada_sem[i], lambda e: e.dma_start(out=wada[i], in_=wada_d[:, g * 2048:(g + 1) * 2048]),
                    writes=[B_wada[i]])
            for q in range(2):
                n = g * 2 + q
                STT(junk, wada[i][:, q * 1024:(q + 1) * 1024], 1.0, cact, ALU.mult, ALU.mult,
                    [B_wada[i], B_cact], [B_junk, B_mod], accum_out=mod[:, n:n + 1])

        def ada_finish(lo, hi):
            TT(mod[:, lo:hi], mod[:, lo:hi], vecs[:, C_BADA + lo:C_BADA + hi], ALU.add, [B_mod, B_vec], [B_mod])

        B_sh = [sch.buf("sh1_%d" % k, O_MOD + k * 4, O_MOD + k * 4 + 4) for k in range(KD)]
        B_sc = [sch.buf("sc1_%d" % k, O_MOD + (8 + k) * 4, O_MOD + (8 + k) * 4 + 4) for k in range(KD)]
        B_gmc = [sch.buf("gm1_%d" % k, O_GM + k * 4, O_GM + k * 4 + 4) for k in range(KD)]
        wada3_d = wada_d.rearrange("p (n k) -> p n k", k=1024)

        def crit_group(k):
            i = wada_i["i"] % NWR
            wada_i["i"] += 1
            dst = wada[i].rearrange("p (n k) -> p n k", n=2)
            sch.dma("sp", wada_sem[i], lambda e, i=i, k=k, dst=dst: e.dma_start(out=dst, in_=wada3_d[:, k:k + 9:8, :]),
                    writes=[B_wada[i]])
            STT(junk, wada[i][:, 0:1024], 1.0, cact, ALU.mult, ALU.mult, [B_wada[i], B_cact], [B_junk, B_sh[k]],
                accum_out=mod[:, k:k + 1])
            STT(junk, wada[i][:, 1024:2048], 1.0, cact, ALU.mult, ALU.mult, [B_wada[i], B_cact], [B_junk, B_sc[k]],
                accum_out=mod[:, 8 + k:9 + k])
            TT(mod[:, k:k + 1], mod[:, k:k + 1], vecs[:, C_BADA + k:C_BADA + k + 1], ALU.add, [B_sh[k], B_vec], [B_sh[k]])
            TT(mod[:, 8 + k:9 + k], mod[:, 8 + k:9 + k], vecs[:, C_BADA + 8 + k:C_BADA + 9 + k], ALU.add,
               [B_sc[k], B_vec], [B_sc[k]])
            sch.op("dve", lambda e, k=k: e.scalar_tensor_tensor(out=gm[:, k:k + 1], in0=mod[:, 8 + k:9 + k], scalar=1.0,
                                                                in1=vecs[:, C_G1 + k:C_G1 + k + 1], op0=ALU.add, op1=ALU.mult),
                   reads=[B_sc[k], B_vec], writes=[B_gmc[k]])

        def norm_sq(r, mode):
            t = NT_U if mode == 1 else NT_N
            for k in range(KD):
                ACT(t.sq[k], xs[:, k * S + r * TU: k * S + (r + 1) * TU], AF.Square, [B_xs[k][r]], [t.B_sq[k]])

        def norm_red(r, mode):
            t = NT_U if mode == 1 else NT_N
            fns = [lambda e, k=k, t=t: e.matmul(bank[6], lhsT=ones_d, rhs=t.sq[k], start=(k == 0), stop=(k == KD - 1))
                   for k in range(KD)]
            sch.pe_group(fns, reads=list(t.B_sq) + [B_cst], writes=[B_bank[6]])
            ACT(t.rstd[r % 2], bank[6], AF.Ln, [B_bank[6], B_vec], [t.B_rstd[r % 2]], bias=vcol(C_EPSR))
            ACT(t.rstd[r % 2], t.rstd[r % 2], AF.Exp, [t.B_rstd[r % 2]], [t.B_rstd[r % 2]], scale=-0.5)

        def norm_stats(r, mode):
            norm_sq(r, mode)
            norm_red(r, mode)

        def norm_apply(r, mode, ks=None):
            t = NT_U if mode == 1 else NT_N
            rstd, B_rstd = t.rstd[r % 2], t.B_rstd[r % 2]
            for k in (range(KD) if ks is None else ks):
                i = k % 2
                xk = xs[:, k * S + r * TU: k * S + (r + 1) * TU]
                if mode == "final":
                    o = ot_i["i"] % NOT
                    ot_i["i"] += 1
                    STT(otile[o], xk, vcol(C_GF + k), rstd, ALU.mult, ALU.mult, [B_xs[k][r], B_rstd, B_vec], [B_ot[o]])
                    sch.dma("sp", otile_sem[o], lambda e, o=o, k=k: e.dma_start(out=outT3[:, k, R(r)], in_=otile[o]),
                            reads=[B_ot[o]])
                else:
                    gcol, shcol = (0, 0) if mode == 1 else (8, 24)
                    hk_ = hbuf[:, k * S + r * TU: k * S + (r + 1) * TU]
                    if mode == 1 or i == 1:
                        STT(t.tmp[i], xk, gm[:, gcol + k: gcol + k + 1], rstd, ALU.mult, ALU.mult,
                            [B_xs[k][r], B_rstd, B_gm], [t.B_tmp[i]])
                        TS(hk_, t.tmp[i], mod[:, shcol + k: shcol + k + 1], ALU.add, [t.B_tmp[i], B_mod], [B_h[k][r]])
                    else:
                        TT(t.tmp[i], xk, rstd, ALU.mult, [B_xs[k][r], B_rstd], [t.B_tmp[i]])
                        ACT(hk_, t.tmp[i], AF.Identity, [t.B_tmp[i], B_gm, B_mod], [B_h[k][r]],
                            bias=mod[:, shcol + k: shcol + k + 1], scale=gm[:, gcol + k: gcol + k + 1])

        def rms_norm(r, mode):
            norm_stats(r, mode)
            norm_apply(r, mode)

        def norm_pipeline(units_fn, mode, fillers=None):
            units_fn(0)
            norm_sq(0, mode)
            units_fn(1)
            norm_red(0, mode)
            norm_sq(1, mode)
            for r in range(2, NR):
                st_a = {"k": 0}

                def one_a(r=r, st_a=st_a):
                    if st_a["k"] < KD:
                        norm_apply(r - 2, mode, ks=[st_a["k"]])
                        st_a["k"] += 1
                units_fn(r, after=one_a)
                while st_a["k"] < KD:
                    one_a()
                norm_red(r - 1, mode)
                norm_sq(r, mode)
            if fillers:
                fillers[0]()
            norm_apply(NR - 2, mode)
            if fillers:
                st_k = {"k": 0, "u": 0, "red": False}

                def one_k():
                    if st_k["k"] < KD:
                        norm_apply(NR - 1, mode, ks=[st_k["k"]])
                        st_k["k"] += 1

                def after_unit():
                    st_k["u"] += 1
                    if st_k["u"] == 2:
                        norm_red(NR - 1, mode)
                        st_k["red"] = True
                    elif st_k["red"]:
                        one_k()
                        if st_k["u"] >= 7:
                            one_k()
                fillers[1](after_unit)
                if not st_k["red"]:
                    norm_red(NR - 1, mode)
                while st_k["k"] < KD:
                    one_k()
            else:
                norm_red(NR - 1, mode)
                norm_apply(NR - 1, mode)

        def checkpoint(n):
            if stop is not None and n == stop:
                raise _Stop()

        def stages():
            checkpoint(0)
            norm_sq(0, 1)
            norm_red(0, 1)
            checkpoint(1)

            def hk(k, r):
                return hbuf[:, k * S + r * TU: k * S + (r + 1) * TU]

            def w8pairs(slot, blk, r):
                return [(slot_ap[slot][:, kt * 512 + blk * 128: kt * 512 + (blk + 1) * 128], hk(kt, r)) for kt in range(KD)]

            def h_reads(r):
                return [B_h[k][r] for k in range(KD)]

            ada_pending = []

            def ada_some(n):
                for _ in range(n):
                    if ada_pending:
                        ada_group(ada_pending.pop(0))

            for jj in range(4):
                sch.op("pool", lambda e, jj=jj: e.memset(upad[:, jj * UW: jj * UW + 32], 0.0), writes=[B_upad0[jj]])
            tU = NT_U
            for k in range(KD):
                crit_group(k)
                i = k % 2
                xk = xs[:, k * S: k * S + TU]
                TT(tU.tmp[i], xk, tU.rstd[0], ALU.mult, [B_xs[k][0], tU.B_rstd[0]], [tU.B_tmp[i]])
                ACT(hk(k, 0), tU.tmp[i], AF.Identity, [tU.B_tmp[i], B_gmc[k], B_sh[k]], [B_h[k][0]],
                    bias=mod[:, k:k + 1], scale=gm[:, k:k + 1])
                for blk8 in range(6):
                    slot = "A" if blk8 < 4 else "B"
                    lt = slot_ap[slot][:, k * 512 + (blk8 % 4) * 128: k * 512 + (blk8 % 4 + 1) * 128]
                    sch.pe_group([lambda e, blk8=blk8, lt=lt, k=k: e.matmul(bank[blk8], lhsT=lt, rhs=hk(k, 0),
                                                                          start=(k == 0), stop=(k == KD - 1))],
                                 reads=[B_h[k][0], slot_buf[slot]], writes=[B_bank[blk8]])
                if k == 2:
                    load_x(1)
                if k == 5:
                    norm_sq(1, 1)
                    norm_red(1, 1)
            load_x(2)
            load_x(3)
            norm_apply(1, 1)
            cnt = 0
            for j in range(3):
                i = cnt % 2
                cnt += 1
                ACT(sig[i], bank[2 * j], AF.Sigmoid, [B_bank[2 * j], B_vec], [B_sig[i]], bias=vcol(C_BGLU + 4 + j))
                STT(upad[:, j * UW + 32: j * UW + 32 + TU], bank[2 * j + 1], vcol(C_BGLU + j), sig[i], ALU.add, ALU.mult,
                    [B_bank[2 * j + 1], B_sig[i], B_vec], [B_u[j][0]])
            b1 = next_bank()
            MM(b1, w8pairs("B", 2, 0), h_reads(0) + [slot_buf["B"]])
            b2 = next_bank()
            MM(b2, w8pairs("B", 3, 0), h_reads(0) + [slot_buf["B"]])
            i = cnt % 2
            cnt += 1
            ACT(sig[i], bank[b1], AF.Sigmoid, [B_bank[b1], B_vec], [B_sig[i]], bias=vcol(C_BGLU + 4 + 3))
            STT(upad[:, 3 * UW + 32: 3 * UW + 32 + TU], bank[b2], vcol(C_BGLU + 3), sig[i], ALU.add, ALU.mult,
                [B_bank[b2], B_sig[i], B_vec], [B_u[3][0]])
            norm_sq(2, 1)
            for _ in range(3):
                ada_some(1)
            for r in range(1, NR):
                for j in range(4):
                    slot = "A" if j < 2 else "B"
                    blk = (j % 2) * 2
                    b1 = next_bank()
                    MM(b1, w8pairs(slot, blk, r), h_reads(r) + [slot_buf[slot]])
                    b2 = next_bank()
                    MM(b2, w8pairs(slot, blk + 1, r), h_reads(r) + [slot_buf[slot]])
                    i = cnt % 2
                    cnt += 1
                    ACT(sig[i], bank[b1], AF.Sigmoid, [B_bank[b1], B_vec], [B_sig[i]], bias=vcol(C_BGLU + 4 + j))
                    STT(upad[:, j * UW + 32 + r * TU: j * UW + 32 + (r + 1) * TU], bank[b2], vcol(C_BGLU + j), sig[i], ALU.add, ALU.mult,
                        [B_bank[b2], B_sig[i], B_vec], [B_u[j][r]])
                    if j < 3:
                        ada_some(1)
                    if j == 0 and r + 1 < NR:
                        norm_red(r + 1, 1)
                        norm_apply(r + 1, 1)
                    if j == 3 and r + 2 < NR:
                        norm_sq(r + 2, 1)
                    if j == 3:
                        if r == 1:
                            build_diag(0)
                            build_diag(1)
                        else:
                            build_diag(r)

            checkpoint(2)
            load_slot("A", w8_d[2])
            load_slot("B", w8_d[3])

            def conv_stats(j):
                sch.pe_group([lambda e, j=j: e.matmul(bank[6], lhsT=ones_c, rhs=cbf[:, j * TU:(j + 1) * TU], start=(j == 0), stop=(j == 3))],
                             reads=[B_cbf[j], B_cst], writes=[B_bank[6]])
                sch.pe_group([lambda e, j=j: e.matmul(bank[7], lhsT=ones_c, rhs=csq[:, j * TU:(j + 1) * TU], start=(j == 0), stop=(j == 3))],
                             reads=[B_csq[j], B_cst], writes=[B_bank[7]])

            for r in range(NR):
                for j in range(4):
                    b = next_bank()
                    pairs = []
                    for k in range(31):
                        if _DBG.get("evenk") and k % 2:
                            continue
                        c0 = j * UW + r * TU + 2 + k
                        pairs.append((diag[:, (j * 31 + k) * 128:(j * 31 + k + 1) * 128], upad[:, c0:c0 + TU]))
                    rd = [B_diag[j], B_u[j][r]] + ([B_u[j][r - 1]] if r > 0 else [B_upad0[j]])
                    MM(b, pairs, rd)
                    ACT(c32[:, j * TU:(j + 1) * TU], bank[b], AF.Identity, [B_bank[b], B_vec], [B_c32[j]], bias=vcol(C_DWB + j))
                    ACT(csq[:, j * TU:(j + 1) * TU], bank[b], AF.Square, [B_bank[b], B_vec], [B_csq[j]], bias=vcol(C_DWB + j))
                    sch.op("dve", lambda e, j=j: e.tensor_copy(out=cbf[:, j * TU:(j + 1) * TU], in_=c32[:, j * TU:(j + 1) * TU]),
                           reads=[B_c32[j]], writes=[B_cbf[j]])
                    if j > 0:
                        conv_stats(j - 1)
                conv_stats(3)
                ACT(mean_sb, bank[6], AF.Identity, [B_bank[6]], [B_mean])
                STT(sqv2, mean_sb, -1.0, mean_sb, ALU.mult, ALU.mult, [B_mean], [B_sqv2])
                TT(sqv2, bank[7], sqv2, ALU.add, [B_bank[7], B_sqv2], [B_sqv2])
                ACT(sqv2, sqv2, AF.Ln, [B_sqv2, B_vec], [B_sqv2], bias=vcol(C_EPSL))
                ACT(rstd2, sqv2, AF.Exp, [B_sqv2], [B_rstd2], scale=-0.5)
                for j in range(4):
                    i = j % 2
                    TT(t1[i], c32[:, j * TU:(j + 1) * TU], mean_sb, ALU.subtract, [B_c32[j], B_mean], [B_t1[i]])
                    TT(t1[i], t1[i], rstd2, ALU.mult, [B_t1[i], B_rstd2], [B_t1[i]])
                    ACT(sbuf_s[:, j * S + r * TU: j * S + (r + 1) * TU], t1[i], AF.Silu, [B_t1[i], B_vec], [B_s[j][r]],
                        bias=vcol(C_LNB + j), scale=vcol(C_LNG + j))

            checkpoint(3)
            O_H1, O_H2 = 61568, 109824
            cactH = f32ap(M + O_H1, 512); B_cactH = mbuf("cactH", O_H1, 2048)
            ringG = [f32ap(M + O_H1 + 2048, 256), f32ap(M + O_H2, 256), f32ap(M + O_H2 + 1024, 256)]
            B_ringG = [mbuf("ringG0", O_H1 + 2048, 1024), mbuf("ringG1", O_H2, 1024), mbuf("ringG2", O_H2 + 1024, 1024)]
            ringG_sem = [dsem("d_ringG%d" % i) for i in range(3)]
            modq = f32ap(M + O_H2 + 2048, 96); B_modq = mbuf("modq", O_H2 + 2048, 384)
            s_cH = dsem("d_cactH")
            bgp = [(h, n, q) for h in (0, 1) for n in range(16, 40) for q in (0, 1)]
            bg = {"dma": 0, "stt": 0, "init": False, "fin": False}

            def bg_half(h):
                sch.dma("sp", s_cH, lambda e, h=h: e.dma_start(out=cactH, in_=cb_d[:, h * 512:(h + 1) * 512]), writes=[B_cactH])
                ACT(cactH, cactH, AF.Silu, [B_cactH], [B_cactH])

            def bg_dma():
                c = bg["dma"]
                if c >= len(bgp) or c - bg["stt"] >= 3:
                    return
                h, n, q = bgp[c]
                if c == 48:
                    sch.dma("sp", s_cH, lambda e: e.dma_start(out=cactH, in_=cb_d[:, 512:1024]), writes=[B_cactH])
                    bg["half1_step"] = bg.get("step", 0)
                    bg["silu_pending"] = True
                bg["dma"] += 1
                i = c % 3
                off = n * 1024 + h * 512 + q * 256
                sch.dma("sp", ringG_sem[i], lambda e, i=i, off=off: e.dma_start(out=ringG[i], in_=wada_d[:, off:off + 256]),
                        writes=[B_ringG[i]])

            def bg_step():
                if bg["fin"]:
                    return
                bg["step"] = bg.get("step", 0) + 1
                if not bg["init"]:
                    bg["init"] = True
                    bg_half(0)
                    sch.op("dve", lambda e: e.memset(modq, 0.0), writes=[B_modq])
                    for _ in range(3):
                        bg_dma()
                    return
                if bg.get("silu_pending") and bg["step"] >= bg["half1_step"] + 1:
                    ACT(cactH, cactH, AF.Silu, [B_cactH], [B_cactH])
                    bg["silu_pending"] = False
                    bg["silu_step"] = bg["step"]
                n_stt = min(3, bg["dma"] - bg["stt"])
                if bg["stt"] >= 48 and (bg.get("silu_pending") or bg["step"] < bg.get("silu_step", 0) + 2):
                    n_stt = 0
                for _ in range(n_stt):
                    c = bg["stt"]
                    bg["stt"] += 1
                    h, n, q = bgp[c]
                    i = c % 3
                    idx = (h * 24 + (n - 16)) * 2 + q
                    STT(bank[7][:, 0:256], ringG[i], 1.0, cactH[:, q * 256:(q + 1) * 256], ALU.mult, ALU.mult,
                        [B_ringG[i], B_cactH], [B_bank[7], B_modq], accum_out=modq[:, idx:idx + 1])
                for _ in range(3):
                    bg_dma()
                if bg["stt"] >= len(bgp):
                    bg["fin"] = True
                    mq = modq.rearrange("p (h n q) -> p h n q", h=2, q=2)
                    TT(mq[:, :, :, 0], mq[:, :, :, 0], mq[:, :, :, 1], ALU.add, [B_modq], [B_modq])
                    TT(mod[:, 16:40], mq[:, 0, :, 0], mq[:, 1, :, 0], ALU.add, [B_modq], [B_mod])
                    ada_finish(16, 40)
                    sch.op("dve", lambda e: e.scalar_tensor_tensor(out=gm[:, 8:16], in0=mod[:, 32:40], scalar=1.0,
                                                                   in1=vecs[:, C_G2:C_G2 + 8], op0=ALU.add, op1=ALU.mult),
                           reads=[B_mod, B_vec], writes=[B_gm])

            for i in range(2):
                sch.op("pool", lambda e, i=i: e.memset(cvp[i][:, 0:2], 0.0), writes=[B_cvp0[i]])
            load_slot("E", w8_d[4])
            load_slot("SWO", w4_d[1])
            load_slot("CWO", w4_d[0])
            sc_slots = ["A", "B", "E"]
            cnt = 0
            for j in range(4):
                cv = cvp[j % 2]
                for r in range(NR):
                    units = []
                    for q in range(3):
                        seq = j * 3 + q
                        units.append((sc_slots[seq // 4], seq % 4))
                    i = cnt % 2
                    cnt += 1
                    b = next_bank()
                    MM(b, w8pairs(units[0][0], units[0][1], r), h_reads(r) + [slot_buf[units[0][0]]])
                    ACT(gcs[i], bank[b], AF.Identity, [B_bank[b]], [B_gcs[i]])
                    b = next_bank()
                    MM(b, w8pairs(units[1][0], units[1][1], r), h_reads(r) + [slot_buf[units[1][0]]])
                    TT(cv[:, 2 + r * TU: 2 + (r + 1) * TU], bank[b], gcs[i], ALU.mult, [B_bank[b], B_gcs[i]], [B_cvp[j % 2][r]])
                    rd = [B_cvp[j % 2][r]] + ([B_cvp[j % 2][r - 1]] if r > 0 else [B_cvp0[j % 2]])
                    TS(acc[i], cv[:, r * TU + 2: r * TU + 2 + TU], vcol(C_SCW + j * 3 + 2), ALU.mult, rd + [B_vec], [B_acc[i]])
                    STT(acc[i], cv[:, r * TU + 1: r * TU + 1 + TU], vcol(C_SCW + j * 3 + 1), acc[i], ALU.mult, ALU.add,
                        rd + [B_vec, B_acc[i]], [B_acc[i]])
                    STT(acc[i], cv[:, r * TU: r * TU + TU], vcol(C_SCW + j * 3), acc[i], ALU.mult, ALU.add,
                        rd + [B_vec, B_acc[i]], [B_acc[i]])
                    b = next_bank()
                    MM(b, w8pairs(units[2][0], units[2][1], r), h_reads(r) + [slot_buf[units[2][0]]])
                    TT(vbuf[:, j * S + r * TU: j * S + (r + 1) * TU], bank[b], acc[i], ALU.mult, [B_bank[b], B_acc[i]], [B_v[j][r]])
                    if j >= 2:
                        bg_step()
                if j == 1:
                    load_slot("A", w8_d[5])
                if j == 2:
                    load_slot("B", w8_d[6])

            checkpoint(4)
            gate_slots = ["A", "B", "A", "B"]
            for m in range(KD):
                gi = m // 2
                slot = gate_slots[gi]
                blkA, blkB = (m % 2) * 2, (m % 2) * 2 + 1
                for r in range(NR):
                    b = next_bank()
                    MM(b, w8pairs(slot, blkA, r), h_reads(r) + [slot_buf[slot]])
                    ACT(gsb[0], bank[b], AF.Sigmoid, [B_bank[b], B_vec], [B_gsb[0]], bias=vcol(C_BGATE + m))
                    b = next_bank()
                    MM(b, [(slot_ap["CWO"][:, kt * 1024 + m * 128: kt * 1024 + (m + 1) * 128], sbuf_s[:, kt * S + r * TU: kt * S + (r + 1) * TU])
                           for kt in range(4)], [B_s[kt][r] for kt in range(4)] + [slot_buf["CWO"]])
                    i = r % 2
                    STT(gy[i], bank[b], vcol(C_CBO + m), gsb[0], ALU.add, ALU.mult, [B_bank[b], B_gsb[0], B_vec], [B_gy[i]])
                    b = next_bank()
                    MM(b, w8pairs(slot, blkB, r), h_reads(r) + [slot_buf[slot]])
                    ACT(gsb[1], bank[b], AF.Sigmoid, [B_bank[b], B_vec], [B_gsb[1]], bias=vcol(C_BGATE + 8 + m))
                    b = next_bank()
                    MM(b, [(slot_ap["SWO"][:, kt * 1024 + m * 128: kt * 1024 + (m + 1) * 128], vbuf[:, kt * S + r * TU: kt * S + (r + 1) * TU])
                           for kt in range(4)], [B_v[kt][r] for kt in range(4)] + [slot_buf["SWO"]])
                    TT(gsb[1], bank[b], gsb[1], ALU.mult, [B_bank[b], B_gsb[1]], [B_gsb[1]])
                    TT(mixed[:, m * S + r * TU: m * S + (r + 1) * TU], gsb[1], gy[i], ALU.add, [B_gsb[1], B_gy[i]], [B_mix[m][r]])
                    bg_step()
                if m == 1:
                    load_slot("A", w8_d[7])
                if m == 3:
                    load_slot("B", w8_d[8])
                if m == 5:
                    load_slot("A", w8_d[9])
            for _ in range(8):
                bg_step()
            assert bg["fin"]
            load_slot("B", w8_d[10])
            load_slot("C", w8_d[11])
            load_slot("D", w8_d[12])

            checkpoint(5)
            def wo_units(r, after=None):
                for m in range(KD):
                    slot = "A" if m < 4 else "B"
                    b = next_bank()
                    MM(b, [(slot_ap[slot][:, kt * 512 + (m % 4) * 128: kt * 512 + (m % 4 + 1) * 128],
                            mixed[:, kt * S + r * TU: kt * S + (r + 1) * TU]) for kt in range(KD)],
                       [B_mix[kt][r] for kt in range(KD)] + [slot_buf[slot]])
                    xk = xs[:, m * S + r * TU: m * S + (r + 1) * TU]
                    STT(xk, bank[b], mod[:, 16 + m:17 + m], xk, ALU.mult, ALU.add, [B_bank[b], B_mod, B_xs[m][r]], [B_xs[m][r]])
                    if after is not None:
                        after()

            ffn_slots = ["C", "D", "A", "B"]
            fs = {"i": 0, "next_gu": 13, "next_wd": 0}
            gu_slot = {}

            O_CF = 102400
            cactF = f32ap(M + O_CF, 1024); B_cactF = mbuf("cactF", O_CF, 4096)
            ringF = [f32ap(M + O_CF + 4096 + i * 2048, 512) for i in range(2)]
            B_ringF = [mbuf("ringF%d" % i, O_CF + 4096 + i * 2048, 2048) for i in range(2)]
            ringF_sem = [dsem("d_ringF%d" % i) for i in range(2)]
            modh2 = f32ap(M + O_CF + 8192, 16); B_modh2 = mbuf("modh2", O_CF + 8192, 64)
            s_cbF = dsem("d_cbF")
            gt2 = {"dma": 0, "stt": 0, "init": False, "fin": False}

            def gt2_dma():
                c = gt2["dma"]
                if c >= 16:
                    return
                gt2["dma"] += 1
                n, h, i = 40 + c // 2, c % 2, c % 2
                sch.dma("sp", ringF_sem[i], lambda e, i=i, n=n, h=h: e.dma_start(
                    out=ringF[i], in_=wada_d[:, n * 1024 + h * 512: n * 1024 + (h + 1) * 512]), writes=[B_ringF[i]])

            def gt2_step():
                if not gt2["init"]:
                    gt2["init"] = True
                    sch.dma("sp", s_cbF, lambda e: e.dma_start(out=cactF, in_=cb_d), writes=[B_cactF])
                    ACT(cactF, cactF, AF.Silu, [B_cactF], [B_cactF])
                    sch.op("dve", lambda e: e.memset(modh2, 0.0), writes=[B_modh2])
                    gt2_dma()
                    gt2_dma()
                    gt2["tick"] = 0
                    return
                gt2["tick"] += 1
                if gt2["tick"] < 3 or gt2["tick"] % 2 == 0:
                    return
                c = gt2["stt"]
                if c < 16:
                    gt2["stt"] += 1
                    h, i = c % 2, c % 2
                    STT(bank[7], ringF[i], 1.0, cactF[:, h * 512:(h + 1) * 512], ALU.mult, ALU.mult,
                        [B_ringF[i], B_cactF], [B_bank[7], B_modh2], accum_out=modh2[:, c:c + 1])
                    gt2_dma()
                elif not gt2["fin"]:
                    gt2["fin"] = True
                    mh = modh2.rearrange("p (n h) -> p n h", h=2)
                    TT(mod[:, 40:48], mh[:, :, 0], mh[:, :, 1], ALU.add, [B_modh2], [B_mod])
                    ada_finish(40, 48)

            gu_done = set()
            fcnt = {"i": 0}

            def gu_unit(g, j, r):
                j0, j1 = FGROUPS[g]
                q = j // 2
                if q not in gu_slot:
                    slot = ffn_slots[fs["i"] % 4]
                    fs["i"] += 1
                    if q >= 2:
                        load_slot(slot, w8_d[11 + q])
                    gu_slot[q] = slot
                slot = gu_slot[q]
                blk = (j % 2) * 2
                i = fcnt["i"] % 2
                fcnt["i"] += 1
                b = next_bank()
                MM(b, w8pairs(slot, blk, r), h_reads(r) + [slot_buf[slot]])
                ACT(sg[i], bank[b], AF.Silu, [B_bank[b]], [B_sg[i]])
                b = next_bank()
                MM(b, w8pairs(slot, blk + 1, r), h_reads(r) + [slot_buf[slot]])
                jj = j - j0
                TT(abuf[g % 2][:, jj * S + r * TU: jj * S + (r + 1) * TU], bank[b], sg[i], ALU.mult,
                   [B_bank[b], B_sg[i]], [B_ab[g % 2][jj][r]])
                gt2_step()
                gu_done.add((j, r))

            def fill0():
                for j in (0, 1):
                    gu_unit(0, j, 0)

            def fill1(after_unit):
                for (j, rr) in [(0, 1), (1, 1), (2, 0), (3, 0), (2, 1), (3, 1), (0, 2), (1, 2)]:
                    gu_unit(0, j, rr)
                    after_unit()

            norm_pipeline(wo_units, 2, fillers=[fill0, fill1])

            checkpoint(6)
            def gu_stage(g):
                j0, j1 = FGROUPS[g]
                if g == 0:
                    order = [(j, r) for r in range(NR) for j in (0, 1)] + [(j, r) for j in range(2, j1) for r in range(NR)]
                else:
                    order = [(j, r) for j in range(j0, j1) for r in range(NR)]
                for (j, r) in order:
                    if (j, r) not in gu_done:
                        gu_unit(g, j, r)

            def down_stage(g, last):
                j0, j1 = FGROUPS[g]
                nk = j1 - j0
                slots = []
                for half in range(2):
                    slot = ffn_slots[fs["i"] % 4]
                    fs["i"] += 1
                    load_slot(slot, wd_d[g * 2 + half], ncols=3072)
                    slots.append(slot)

                def unit(m, r):
                    slot = slots[m // 4]
                    b = next_bank()
                    MM(b, [(slot_ap[slot][:, kk * 512 + (m % 4) * 128: kk * 512 + (m % 4 + 1) * 128],
                            abuf[g % 2][:, kk * S + r * TU: kk * S + (r + 1) * TU]) for kk in range(nk)],
                       [B_ab[g % 2][kk][r] for kk in range(nk)] + [slot_buf[slot]])
                    xk = xs[:, m * S + r * TU: m * S + (r + 1) * TU]
                    STT(xk, bank[b], mod[:, 40 + m:41 + m], xk, ALU.mult, ALU.add, [B_bank[b], B_mod, B_xs[m][r]], [B_xs[m][r]])

                if not last:
                    for m in range(KD):
                        for r in range(NR):
                            unit(m, r)
                else:
                    def last_units(r, after=None):
                        for m in range(KD):
                            unit(m, r)
                            if after is not None:
                                after()
                    norm_pipeline(last_units, "final")

            gu_stage(0)
            gu_stage(1)
            for _ in range(80):
                if gt2["fin"]:
                    break
                gt2_step()
            assert gt2["fin"]
            down_stage(0, False)
            gu_stage(2)
            down_stage(1, False)
            gu_stage(3)
            down_stage(2, False)
            down_stage(3, True)


        try:
            stages()
        except _Stop:
            for k in range(KD):
                for r in range(NR):
                    i = (k * NR + r) % 2
                    sch.dma("sp", otile_sem[i], lambda e, k=k, r=r: e.dma_start(
                        out=outT3[:, k, R(r)], in_=xs[:, k * S + r * TU: k * S + (r + 1) * TU]), reads=[B_xs[k][r]])

        for i in range(NOT):
            if sch.dma_cnt.get(otile_sem[i], 0):
                sch.wait_tok("sp", (otile_sem[i], sch.dma_cnt[otile_sem[i]]))
        sch.emit()
    return nc


def _chunk(v):
    return np.ascontiguousarray(np.asarray(v, np.float32).reshape(-1, 128).T)


def _unit(W, col_starts, kpad=None):
    KT = W.shape[0] // 128
    Wr = W.reshape(KT, 128, W.shape[1])
    U = np.concatenate([Wr[:, :, c:c + 128] for c in col_starts], axis=2)
    U = U.transpose(1, 0, 2)
    if kpad is not None and kpad > KT:
        U = np.concatenate([U, np.zeros((128, kpad - KT, U.shape[2]), np.float32)], axis=1)
    return np.ascontiguousarray(U).reshape(128, -1)


_NC_CACHE = {}


def _prep(x, c, w_ada, b_ada, norm1_g, w_in, b_glu, conf_dw_w, conf_dw_b, conf_ln_g, conf_ln_b, conf_w_out,
          conf_b_out, sc_dw_w, sc_w_out, b_gate, w_o, norm2_g, w_gu, w_down, final_g, cores=None):
    f = lambda a: np.asarray(a, np.float32)
    x, c = f(x), f(c)
    w_ada, w_in, w_o, w_gu, w_down = f(w_ada)[0], f(w_in)[0], f(w_o)[0], f(w_gu)[0], f(w_down)[0]
    conf_w_out, sc_w_out = f(conf_w_out)[0], f(sc_w_out)[0]

    vecs = np.zeros((128, NV), np.float32)
    vecs[:, C_BADA:C_BADA + 48] = _chunk(f(b_ada)[0])
    vecs[:, C_G1:C_G1 + 8] = _chunk(f(norm1_g)[0])
    vecs[:, C_BGLU:C_BGLU + 8] = _chunk(f(b_glu)[0])
    vecs[:, C_DWB:C_DWB + 4] = _chunk(f(conf_dw_b)[0])
    vecs[:, C_LNG:C_LNG + 4] = _chunk(f(conf_ln_g)[0])
    vecs[:, C_LNB:C_LNB + 4] = _chunk(f(conf_ln_b)[0])
    vecs[:, C_CBO:C_CBO + 8] = _chunk(f(conf_b_out)[0])
    vecs[:, C_BGATE:C_BGATE + 16] = _chunk(f(b_gate)[0])
    vecs[:, C_G2:C_G2 + 8] = _chunk(f(norm2_g)[0])
    vecs[:, C_GF:C_GF + 8] = _chunk(f(final_g))
    dww = f(conf_dw_w)[0]
    vecs[:, C_DWW:C_DWW + 124] = dww.reshape(31, 4, 128).transpose(2, 1, 0).reshape(128, 124)
    scw = f(sc_dw_w)[0]
    vecs[:, C_SCW:C_SCW + 12] = scw.reshape(3, 4, 128).transpose(2, 1, 0).reshape(128, 12)
    vecs[:, C_EPSR] = 1e-6
    vecs[:, C_EPSL] = 1e-5

    cst = np.zeros((128, 384), np.float32)
    cst[:, 0:128] = np.eye(128, dtype=np.float32)
    cst[:, 128:256] = 1.0 / 1024.0
    cst[:, 256:384] = 1.0 / 512.0

    wada = np.ascontiguousarray(w_ada.reshape(1024, 48, 128).transpose(2, 1, 0)).reshape(128, 48 * 1024)

    ua = lambda j: 128 * j
    ub = lambda j: 512 + 128 * j
    sb = lambda j: 1024 + 128 * j
    gc = lambda j: 1536 + 128 * j
    vv = lambda j: 2048 + 128 * j
    gA = lambda m: 2560 + 128 * m
    gB = lambda m: 3584 + 128 * m
    units = []
    units.append(_unit(w_in, [ub(0), ua(0), ub(1), ua(1)]))
    units.append(_unit(w_in, [ub(2), ua(2), ub(3), ua(3)]))
    seq = []
    for j in range(4):
        seq += [gc(j), vv(j), sb(j)]
    for i in range(3):
        units.append(_unit(w_in, seq[i * 4:(i + 1) * 4]))
    for i in range(4):
        units.append(_unit(w_in, [gA(2 * i), gB(2 * i), gA(2 * i + 1), gB(2 * i + 1)]))
    units.append(_unit(w_o, [0, 128, 256, 384]))
    units.append(_unit(w_o, [512, 640, 768, 896]))
    for q in range(11):
        units.append(_unit(w_gu, [128 * (2 * q), 2816 + 128 * (2 * q), 128 * (2 * q + 1), 2816 + 128 * (2 * q + 1)]))
    w8 = np.stack(units, axis=0)
    w4 = np.stack([_unit(conf_w_out, [128 * m for m in range(8)]), _unit(sc_w_out, [128 * m for m in range(8)])], axis=0)
    wdu = []
    for (j0, j1) in FGROUPS:
        for half in range(2):
            wdu.append(_unit(w_down[j0 * 128:j1 * 128], [half * 512 + 128 * mm for mm in range(4)], kpad=6))
    wd = np.stack(wdu, axis=0)

    in_maps = []
    for b in (range(NCORES) if cores is None else cores):
        in_maps.append({
            "xT": np.ascontiguousarray(x[b].T),
            "cb": np.ascontiguousarray(np.broadcast_to(c[b][None, :], (128, 1024))),
            "wada": wada, "vecs": vecs, "cst": cst, "w8": w8, "w4": w4, "wd": wd,
        })
    return in_maps


def kernel(**inputs):
    in_maps = _prep(**inputs)
    if "nc" not in _NC_CACHE:
        _NC_CACHE["nc"] = build_nc()
    nc = _NC_CACHE["nc"]
    res = run_bass_kernel_spmd(nc, in_maps, core_ids=list(range(NCORES)))
    out = np.stack([np.ascontiguousarray(res.results[b]["outT"].T) for b in range(NCORES)], axis=0)
    return out.astype(np.float32)
```
